# Optimizing a Trainium2 kernel written in Bass

```python
import jax
import jax.numpy as jnp
from jax import lax
import numpy as np


D_MODEL = 1024
BATCH = 4
SEQ = 8192
DEPTH = 2
DEC_BATCH = 1
DEC_SEQ = 16384
PAST_LEN = 128

PLE_DIM = 256
GRID_W = 64
N_MIXERS = 4
D_MIX = D_MODEL
W_GROUP = D_MIX // N_MIXERS
HEAD_DIM = 64
N_HEADS_GROUP = W_GROUP // HEAD_DIM
LRU_BLOCK = W_GROUP // N_HEADS_GROUP
LRU_C = 8.0
SHORT_CONV = 4
SHORT_CONV_LEFT = 2
CHUNK = 64
ROPE_BASE = 10000.0
NA_KH = 8
NA_KW = 16
D_FF = 2816
FFN_CONV = 3
EPS = 1e-6
SPLIT_SIZES = (W_GROUP, W_GROUP, 3 * W_GROUP, W_GROUP, 2 * N_HEADS_GROUP, 2 * N_HEADS_GROUP, 3 * W_GROUP, W_GROUP, 3 * W_GROUP)
IN_COLS = 13 * W_GROUP + 4 * N_HEADS_GROUP

kernel_name = 'hybrid_bidir_encoder_parallel_heads'


def rms_norm(x, gain):
    xf = x.astype(jnp.float32)
    y = xf * lax.rsqrt(jnp.mean(xf * xf, axis=-1, keepdims=True) + EPS)
    return (y * gain.astype(jnp.float32)).astype(x.dtype)


def head_layer_norm(t, gain):
    mu = jnp.mean(t, axis=-1, keepdims=True)
    tc = t - mu
    return tc * lax.rsqrt(jnp.mean(tc * tc, axis=-1, keepdims=True) + EPS) * gain


def l2_normalize(t):
    return t * lax.rsqrt(jnp.sum(t * t, axis=-1, keepdims=True) + EPS)


def depthwise_conv(x, w, left):
    k = w.shape[0]
    s = x.shape[1]
    xp = jnp.pad(x, ((0, 0), (left, k - 1 - left), (0, 0)))
    out = xp[:, 0:s] * w[0]
    for j in range(1, k):
        out = out + xp[:, j:j + s] * w[j]
    return out


def flip_seq(t):
    return jnp.flip(t, axis=1)


def to_chunks(t):
    b, s, h = t.shape[:3]
    return jnp.moveaxis(t.reshape(b, s // CHUNK, CHUNK, h, -1), 3, 1)


def from_chunks(t):
    b, h, n, c, d = t.shape
    return jnp.moveaxis(t, 1, 3).reshape(b, n * c, h, d)


def rotary(t):
    s, d = t.shape[1], t.shape[-1]
    half = d // 2
    inv_freq = ROPE_BASE ** (-jnp.arange(half, dtype=jnp.float32) / half)
    ang = jnp.arange(s, dtype=jnp.float32)[:, None] * inv_freq[None, :]
    cos = jnp.cos(ang)[None, :, None, :]
    sin = jnp.sin(ang)[None, :, None, :]
    t1, t2 = t[..., :half], t[..., half:]
    return jnp.concatenate([t1 * cos - t2 * sin, t1 * sin + t2 * cos], axis=-1)


def linear_combine(left, right):
    a_l, b_l = left
    a_r, b_r = right
    return a_l * a_r, a_r * b_l + b_r


def rglru_direction(x, w_r, b_r, w_i, b_i, lam, reverse):
    b, s, w = x.shape
    xb = x.reshape(b, s, N_HEADS_GROUP, LRU_BLOCK)
    r = jax.nn.sigmoid(jnp.einsum('bshi,hij->bshj', xb, w_r).reshape(b, s, w) + b_r)
    i = jax.nn.sigmoid(jnp.einsum('bshi,hij->bshj', xb, w_i).reshape(b, s, w) + b_i)
    log_a = -LRU_C * r * jax.nn.softplus(-lam)
    gated_x = jnp.sqrt(-jnp.expm1(2.0 * log_a)) * (i * x)
    _, h = lax.associative_scan(linear_combine, (jnp.exp(log_a), gated_x), reverse=reverse, axis=1)
    return h


def gated_delta_chunked(q, k, v, beta, g):
    q, k, v = to_chunks(q), to_chunks(k), to_chunks(v)
    beta = to_chunks(beta[..., None])[..., 0]
    gcum = jnp.cumsum(to_chunks(g[..., None])[..., 0], axis=-1)
    c = CHUNK
    lower = jnp.tril(jnp.ones((c, c), dtype=bool))
    strict = jnp.tril(jnp.ones((c, c), dtype=bool), -1)
    decay = jnp.exp(jnp.where(lower, gcum[..., :, None] - gcum[..., None, :], -jnp.inf))
    kk = jnp.einsum('bhnid,bhnjd->bhnij', k, k)
    l_mat = jnp.where(strict, beta[..., :, None] * kk * decay, 0.0)
    dv = v.shape[-1]
    rhs = jnp.concatenate([v * beta[..., None], k * (beta * jnp.exp(gcum))[..., None]], axis=-1)
    sol = lax.linalg.triangular_solve(l_mat, rhs, left_side=True, lower=True, unit_diagonal=True)
    u_val, w_key = sol[..., :dv], sol[..., dv:]
    attn = jnp.einsum('bhnid,bhnjd->bhnij', q, k) * decay
    q_g = q * jnp.exp(gcum)[..., None]
    g_last = gcum[..., -1]
    k_g = k * jnp.exp(g_last[..., None] - gcum)[..., None]

    def step(state, xs):
        u_n, w_n, q_n, a_n, k_n, gl_n = xs
        v_new = u_n - jnp.einsum('bhck,bhkv->bhcv', w_n, state)
        o_n = jnp.einsum('bhck,bhkv->bhcv', q_n, state) + jnp.einsum('bhij,bhjv->bhiv', a_n, v_new)
        state = state * jnp.exp(gl_n)[..., None, None] + jnp.einsum('bhck,bhcv->bhkv', k_n, v_new)
        return state, o_n

    xs = tuple(jnp.moveaxis(t, 2, 0) for t in (u_val, w_key, q_g, attn, k_g, g_last))
    b, h, _, _, dk = q.shape
    state0 = jnp.zeros((b, h, dk, dv), q.dtype)
    _, o = lax.scan(step, state0, xs)
    return from_chunks(jnp.moveaxis(o, 0, 2))


def retention_chunked(q, k, v, log_gamma):
    q, k, v = to_chunks(q), to_chunks(k), to_chunks(v)
    c = CHUNK
    pos = jnp.arange(c, dtype=jnp.float32)
    lower = jnp.tril(jnp.ones((c, c), dtype=bool))
    lg = log_gamma[:, None]
    dmask = jnp.exp(jnp.where(lower, (pos[:, None] - pos[None, :]) * log_gamma[:, None, None], -jnp.inf))
    scores = jnp.einsum('bhnid,bhnjd->bhnij', q, k) * dmask[None, :, None]
    inner = jnp.einsum('bhnij,bhnje->bhnie', scores, v)
    q_dec = q * jnp.exp((pos[None, :] + 1.0) * lg)[None, :, None, :, None]
    k_dec = k * jnp.exp((c - 1.0 - pos[None, :]) * lg)[None, :, None, :, None]
    kv = jnp.einsum('bhncd,bhnce->nbhde', k_dec, v)
    chunk_decay = jnp.exp(c * log_gamma)[None, :, None, None]

    def step(state, kv_n):
        return state * chunk_decay + kv_n, state

    _, prev = lax.scan(step, jnp.zeros_like(kv[0]), kv)
    cross = jnp.einsum('bhncd,nbhde->bhnce', q_dec, prev)
    return from_chunks(inner + cross)


def neighborhood_attention(q, k, v, rpb):
    b, s, h, d = q.shape
    rows = s // GRID_W
    kh = min(NA_KH, rows)
    qg = q.reshape(b, rows, GRID_W, h, d)
    kg = k.reshape(b, rows, GRID_W, h, d)
    vg = v.reshape(b, rows, GRID_W, h, d)
    cols = jnp.arange(GRID_W)
    c0 = jnp.clip(cols - NA_KW // 2, 0, GRID_W - NA_KW)
    col_idx = c0[:, None] + jnp.arange(NA_KW)[None, :]
    dc = col_idx - cols[:, None] + (NA_KW - 1)
    scale = d ** -0.5

    def row_block(r):
        r0 = jnp.clip(r - kh // 2, 0, rows - kh)
        k_win = lax.dynamic_slice_in_dim(kg, r0, kh, axis=1)[:, :, col_idx]
        v_win = lax.dynamic_slice_in_dim(vg, r0, kh, axis=1)[:, :, col_idx]
        q_r = lax.dynamic_index_in_dim(qg, r, axis=1, keepdims=False)
        dr = r0 + jnp.arange(kh) - r + (NA_KH - 1)
        bias = rpb[:, dr[:, None, None], dc[None, :, :]]
        scores = jnp.einsum('bchd,bkcwhd->bhckw', q_r, k_win) * scale + jnp.transpose(bias, (0, 2, 1, 3))[None]
        probs = jax.nn.softmax(scores.reshape(b, h, GRID_W, kh * NA_KW), axis=-1).reshape(scores.shape)
        return jnp.einsum('bhckw,bkcwhd->bchd', probs, v_win)

    out = lax.map(row_block, jnp.arange(rows))
    return jnp.moveaxis(out, 0, 1).reshape(b, s, h, d)


def encoder_layer(x, p_l, lw):
    b, s, _ = x.shape
    f32 = jnp.float32
    heads = lambda t: t.reshape(b, s, N_HEADS_GROUP, HEAD_DIM)
    hn = rms_norm(x, lw['norm1'])
    u = (hn @ lw['w_in']).astype(f32)
    xa, ya, qkv_b, z_b, beta_b, alpha_b, qkv_c, gate_c, qkv_d = jnp.split(u, np.cumsum(SPLIT_SIZES)[:-1].tolist(), axis=-1)

    xa = depthwise_conv(xa, lw['conv_a_w'].astype(f32), SHORT_CONV_LEFT) + lw['conv_a_b'].astype(f32)
    wr, br, wi, bi, lam = (lw[n].astype(f32) for n in ('lru_wr', 'lru_br', 'lru_wi', 'lru_bi', 'lru_lambda'))
    h_a = rglru_direction(xa, wr[0], br[0], wi[0], bi[0], lam[0], False) + rglru_direction(xa, wr[1], br[1], wi[1], bi[1], lam[1], True)
    out_a = h_a * jax.nn.gelu(ya)

    qkv_b = jax.nn.silu(depthwise_conv(qkv_b, lw['gdn_conv'].astype(f32), SHORT_CONV_LEFT))
    q_b, k_b, v_b = (heads(t) for t in jnp.split(qkv_b, 3, axis=-1))
    q_b = l2_normalize(q_b) * (HEAD_DIM ** -0.5)
    k_b = l2_normalize(k_b)
    beta = jax.nn.sigmoid(beta_b).reshape(b, s, 2, N_HEADS_GROUP)
    g = -jnp.exp(lw['gdn_a_log'].astype(f32)) * jax.nn.softplus(alpha_b.reshape(b, s, 2, N_HEADS_GROUP) + lw['gdn_dt_bias'].astype(f32))
    o_b = gated_delta_chunked(q_b, k_b, v_b, beta[:, :, 0], g[:, :, 0]) + flip_seq(gated_delta_chunked(flip_seq(q_b), flip_seq(k_b), flip_seq(v_b), flip_seq(beta[:, :, 1]), flip_seq(g[:, :, 1])))
    out_b = rms_norm(o_b, lw['gdn_norm']) * jax.nn.silu(heads(z_b))

    q_c, k_c, v_c = (heads(t) for t in jnp.split(qkv_c, 3, axis=-1))
    q_c = rotary(q_c)
    k_c = rotary(k_c) * (HEAD_DIM ** -0.5)
    log_gamma = jnp.log1p(-jnp.exp2(-lw['ret_decay'].astype(f32)))
    o_c = retention_chunked(q_c, k_c, v_c, log_gamma[0]) + flip_seq(retention_chunked(flip_seq(q_c), flip_seq(k_c), flip_seq(v_c), log_gamma[1]))
    out_c = head_layer_norm(o_c, lw['ret_norm'].astype(f32)) * jax.nn.silu(heads(gate_c))

    q_d, k_d, v_d = (heads(t) for t in jnp.split(qkv_d, 3, axis=-1))
    q_d = rms_norm(q_d, lw['na_qnorm'])
    k_d = rms_norm(k_d, lw['na_knorm'])
    out_d = neighborhood_attention(q_d, k_d, v_d, lw['na_rpb'].astype(f32))

    mix = jnp.concatenate([out_a, out_b.reshape(b, s, W_GROUP), out_c.reshape(b, s, W_GROUP), out_d.reshape(b, s, W_GROUP)], axis=-1)
    x = x + mix.astype(x.dtype) @ lw['w_out']

    h2 = rms_norm(x, lw['norm2'])
    gate = depthwise_conv(h2 @ lw['ffn_wg'], lw['ffn_conv_w'], FFN_CONV // 2) + lw['ffn_conv_b']
    x = x + (jax.nn.gelu(gate) * (h2 @ lw['ffn_wu'])) @ lw['ffn_wd']

    h3 = rms_norm(x, lw['norm3'])
    x = x + jax.nn.sigmoid(h3 @ lw['ple_gate']) * (p_l @ lw['ple_proj'])
    return x


def setup_inputs(seed: int = 0) -> dict:
    key = jax.random.key(seed)
    ks = iter(jax.random.split(key, 40))
    H = N_HEADS_GROUP

    def nrm(shape, scale):
        return jax.random.normal(next(ks), shape, jnp.float32) * scale

    def gain(shape):
        return 1.0 + nrm(shape, 0.02)

    x_prompt = nrm((BATCH, SEQ, D_MODEL), 1.0)
    x_sample = nrm((DEC_BATCH, DEC_SEQ, D_MODEL), 1.0)
    p_prompt = nrm((DEPTH, BATCH, SEQ, PLE_DIM), 1.0)
    p_sample = nrm((DEPTH, DEC_BATCH, DEC_SEQ, PLE_DIM), 1.0)
    norm1 = gain((DEPTH, D_MODEL))
    norm2 = gain((DEPTH, D_MODEL))
    norm3 = gain((DEPTH, D_MODEL))
    w_in = nrm((DEPTH, D_MODEL, IN_COLS), D_MODEL ** -0.5)
    w_out = nrm((DEPTH, D_MIX, D_MODEL), D_MIX ** -0.5)
    conv_a_w = nrm((DEPTH, SHORT_CONV, W_GROUP), SHORT_CONV ** -0.5)
    conv_a_b = nrm((DEPTH, W_GROUP), 0.02)
    lru_wr = nrm((DEPTH, 2, H, LRU_BLOCK, LRU_BLOCK), LRU_BLOCK ** -0.5)
    lru_br = nrm((DEPTH, 2, W_GROUP), 0.02)
    lru_wi = nrm((DEPTH, 2, H, LRU_BLOCK, LRU_BLOCK), LRU_BLOCK ** -0.5)
    lru_bi = nrm((DEPTH, 2, W_GROUP), 0.02)
    a_target = jax.random.uniform(next(ks), (DEPTH, 2, W_GROUP), jnp.float32, minval=0.9, maxval=0.999)
    sig = a_target ** (1.0 / LRU_C)
    lru_lambda = jnp.log(sig) - jnp.log1p(-sig)
    gdn_conv = nrm((DEPTH, SHORT_CONV, 3 * W_GROUP), SHORT_CONV ** -0.5)
    gdn_a_log = jnp.log(jax.random.uniform(next(ks), (DEPTH, 2, H), jnp.float32, minval=1.0, maxval=16.0))
    dt = jnp.exp(jax.random.uniform(next(ks), (DEPTH, 2, H), jnp.float32, minval=float(np.log(1e-3)), maxval=float(np.log(1e-1))))
    gdn_dt_bias = dt + jnp.log(-jnp.expm1(-dt))
    gdn_norm = gain((DEPTH, HEAD_DIM))
    ret_decay = 5.0 + jnp.arange(H, dtype=jnp.float32)[None, None, :] + nrm((DEPTH, 2, H), 0.1)
    ret_norm = gain((DEPTH, HEAD_DIM))
    na_qnorm = gain((DEPTH, HEAD_DIM))
    na_knorm = gain((DEPTH, HEAD_DIM))
    na_rpb = nrm((DEPTH, H, 2 * NA_KH - 1, 2 * NA_KW - 1), 0.02)
    ffn_wg = nrm((DEPTH, D_MODEL, D_FF), D_MODEL ** -0.5)
    ffn_wu = nrm((DEPTH, D_MODEL, D_FF), D_MODEL ** -0.5)
    ffn_conv_w = nrm((DEPTH, FFN_CONV, D_FF), FFN_CONV ** -0.5)
    ffn_conv_b = nrm((DEPTH, D_FF), 0.02)
    ffn_wd = nrm((DEPTH, D_FF, D_MODEL), D_FF ** -0.5)
    ple_proj = nrm((DEPTH, PLE_DIM, D_MODEL), PLE_DIM ** -0.5)
    ple_gate = nrm((DEPTH, D_MODEL, D_MODEL), D_MODEL ** -0.5)
    return {'x_prompt': x_prompt, 'x_sample': x_sample, 'p_prompt': p_prompt, 'p_sample': p_sample,
            'norm1': norm1, 'norm2': norm2, 'norm3': norm3, 'w_in': w_in, 'w_out': w_out,
            'conv_a_w': conv_a_w, 'conv_a_b': conv_a_b, 'lru_wr': lru_wr, 'lru_br': lru_br,
            'lru_wi': lru_wi, 'lru_bi': lru_bi, 'lru_lambda': lru_lambda,
            'gdn_conv': gdn_conv, 'gdn_a_log': gdn_a_log, 'gdn_dt_bias': gdn_dt_bias, 'gdn_norm': gdn_norm,
            'ret_decay': ret_decay, 'ret_norm': ret_norm,
            'na_qnorm': na_qnorm, 'na_knorm': na_knorm, 'na_rpb': na_rpb,
            'ffn_wg': ffn_wg, 'ffn_wu': ffn_wu, 'ffn_conv_w': ffn_conv_w, 'ffn_conv_b': ffn_conv_b, 'ffn_wd': ffn_wd,
            'ple_proj': ple_proj, 'ple_gate': ple_gate}


def reference(x_prompt, x_sample, p_prompt, p_sample, norm1, norm2, norm3, w_in, w_out,
              conv_a_w, conv_a_b, lru_wr, lru_br, lru_wi, lru_bi, lru_lambda,
              gdn_conv, gdn_a_log, gdn_dt_bias, gdn_norm, ret_decay, ret_norm,
              na_qnorm, na_knorm, na_rpb, ffn_wg, ffn_wu, ffn_conv_w, ffn_conv_b, ffn_wd,
              ple_proj, ple_gate):
    def run_trunk(x, p):
        for i in range(DEPTH):
            lw = {'norm1': norm1[i], 'norm2': norm2[i], 'norm3': norm3[i], 'w_in': w_in[i], 'w_out': w_out[i],
                  'conv_a_w': conv_a_w[i], 'conv_a_b': conv_a_b[i], 'lru_wr': lru_wr[i], 'lru_br': lru_br[i],
                  'lru_wi': lru_wi[i], 'lru_bi': lru_bi[i], 'lru_lambda': lru_lambda[i],
                  'gdn_conv': gdn_conv[i], 'gdn_a_log': gdn_a_log[i], 'gdn_dt_bias': gdn_dt_bias[i], 'gdn_norm': gdn_norm[i],
                  'ret_decay': ret_decay[i], 'ret_norm': ret_norm[i],
                  'na_qnorm': na_qnorm[i], 'na_knorm': na_knorm[i], 'na_rpb': na_rpb[i],
                  'ffn_wg': ffn_wg[i], 'ffn_wu': ffn_wu[i], 'ffn_conv_w': ffn_conv_w[i], 'ffn_conv_b': ffn_conv_b[i],
                  'ffn_wd': ffn_wd[i], 'ple_proj': ple_proj[i], 'ple_gate': ple_gate[i]}
            x = encoder_layer(x, p[i], lw)
        return x

    y_prompt = run_trunk(x_prompt, p_prompt)
    y_sample = run_trunk(x_sample, p_sample)
    return (y_prompt, y_sample)
```

```python
import types
import numpy as np
from contextlib import ExitStack
import concourse.bass as bass
import concourse.mybir as mybir
from concourse.bass_utils import run_bass_kernel_spmd

F32 = mybir.dt.float32
BF16 = mybir.dt.bfloat16
AF = mybir.ActivationFunctionType
ALU = mybir.AluOpType

D = 1024
DEPTH = 2
WG = 256
HD = 64
NH = 4
DFF = 2816
NJ = DFF // 128
PLE = 256
EPS = 1e-6
NEG = -30000.0
NFM = 26
NTMC = 1040
WCOLS = NFM * 128 + NTMC


class Buf:
    __slots__ = ("name", "w", "r")

    def __init__(self, name):
        self.name = name
        self.w = None
        self.r = []


class Op:
    __slots__ = ("eng", "fn", "deps", "need_inc", "count", "is_dma", "slot", "val", "idx")


def _freeze(fn):
    if fn.__closure__ is None:
        return fn
    cells = []
    for c in fn.__closure__:
        try:
            cells.append(types.CellType(c.cell_contents))
        except ValueError:
            cells.append(c)
    return types.FunctionType(fn.__code__, fn.__globals__, fn.__name__, fn.__defaults__, tuple(cells))


class Sched:
    ENG = ("pe", "act", "dve", "pool", "sp")

    def __init__(self, nc):
        self.nc = nc
        self.ops = {e: [] for e in self.ENG}
        self.nops = 0

    limit = None
    trace_names = None

    def op(self, eng, fn, reads=(), writes=(), dma=False):
        if self.limit is not None and self.nops >= self.limit:
            return None
        if self.trace_names is not None:
            self.trace_names.append((self.nops, eng, fn.__code__.co_firstlineno))
        o = Op()
        o.eng = eng
        o.fn = _freeze(fn)
        o.is_dma = dma
        o.need_inc = dma
        o.count = 0
        o.slot = o.val = None
        o.idx = self.nops
        self.nops += 1
        deps = set()
        for b in reads:
            if b.w is not None:
                deps.add(b.w)
        for b in writes:
            if b.w is not None:
                deps.add(b.w)
            for r in b.r:
                deps.add(r)
        deps.discard(o)
        o.deps = deps
        for b in writes:
            b.w = o
            b.r = []
        for b in reads:
            if b.w is not o:
                b.r.append(o)
        self.ops[eng].append(o)
        return o

    def pe(self, fn, r=(), w=()):
        return self.op("pe", fn, r, w)

    def act(self, fn, r=(), w=()):
        return self.op("act", fn, r, w)

    def dve(self, fn, r=(), w=()):
        return self.op("dve", fn, r, w)

    def pool(self, fn, r=(), w=()):
        return self.op("pool", fn, r, w)

    def load(self, fn, r=(), w=()):
        return self.op("sp", fn, r, w, dma=True)

    def store(self, fn, r=(), w=()):
        return self.op("pool", fn, r, w, dma=True)

    def emit(self, stack):
        nc = self.nc
        NS = 12
        sems = {e: stack.enter_context(nc.semaphore("s_" + e)) for e in ("pe", "act", "dve", "pool")}
        dsem = {q: [stack.enter_context(nc.semaphore("d_%s%d" % (q, i))) for i in range(NS)] for q in ("sp", "pool")}
        for e in self.ENG:
            for o in self.ops[e]:
                for d in o.deps:
                    if d.eng == "pe" and o.eng == "pe":
                        continue
                    d.need_inc = True
        for e in self.ENG:
            c = 0
            nd = 0
            for o in self.ops[e]:
                if o.is_dma:
                    o.slot = nd % NS
                    o.val = 16 * (nd // NS + 1)
                    nd += 1
                elif o.need_inc:
                    c += 1
                    o.count = c
        handles = {"pe": nc.tensor, "act": nc.scalar, "dve": nc.vector, "pool": nc.gpsimd, "sp": nc.sync}
        block = stack.enter_context(nc.Block())

        def body(e):
            h = handles[e]
            waited = {}
            last_dma = {}
            for o in self.ops[e]:
                need = {}
                for d in o.deps:
                    if d.is_dma:
                        key = ("d", d.eng, d.slot)
                        v = d.val
                    else:
                        if d.eng == "pe" and e == "pe":
                            continue
                        key = ("c", d.eng)
                        v = d.count
                    if need.get(key, 0) < v:
                        need[key] = v
                if o.is_dma and o.val > 16:
                    key = ("d", e, o.slot)
                    if need.get(key, 0) < o.val - 16:
                        need[key] = o.val - 16
                for key, v in need.items():
                    if waited.get(key, 0) >= v:
                        continue
                    waited[key] = v
                    s = dsem[key[1]][key[2]] if key[0] == "d" else sems[key[1]]
                    h.wait_ge(s, v)
                ins = o.fn()
                if o.is_dma:
                    ins.then_inc(dsem[e][o.slot], 16)
                    last_dma[o.slot] = o.val
                elif o.need_inc:
                    ins.then_inc(sems[e], 1)
            for slot, v in last_dma.items():
                h.wait_ge(dsem[e][slot], v)

        @block.sync
        def _(x):
            body("sp")

        @block.tensor
        def _(x):
            body("pe")

        @block.scalar
        def _(x):
            body("act")

        @block.vector
        def _(x):
            body("dve")

        @block.gpsimd
        def _(x):
            body("pool")


def _win_cols():
    o_xa, o_ya, o_qkvb, o_zb, o_beta, o_alpha, o_qkvc, o_gc, o_qkvd = 0, 256, 512, 1280, 1536, 1544, 1552, 2320, 2576
    r = np.arange(256)
    swap = (r // 64) * 64 + ((r % 64) + 32) % 64
    fm = np.concatenate([
        o_xa + r, o_ya + r, o_qkvb + r, o_qkvb + 256 + r, o_qkvb + 512 + r, o_zb + r,
        o_qkvc + r, o_qkvc + swap, o_qkvc + 256 + r, o_qkvc + 256 + swap, o_gc + r,
        o_qkvd + r, o_qkvd + 256 + r])
    tm = np.concatenate([
        o_qkvc + 512 + r, o_qkvd + 512 + r, o_qkvc + 256 + r, o_qkvc + 256 + swap,
        o_beta + np.arange(8), o_alpha + np.arange(8)])
    return np.concatenate([fm, tm])


def _na_windows(SEG):
    PS = SEG // 128
    NT = 2 * PS
    out = []
    for p in range(NT):
        seg, lp = divmod(p, PS)
        if lp in (0, 1):
            cls = 1 + seg * 4 + lp
        elif lp in (PS - 2, PS - 1):
            cls = 1 + seg * 4 + 2 + (lp - (PS - 2))
        else:
            cls = 0
        lo = min(max(p - 2, 0), NT - 5)
        n = 5
        if seg == 1 and lp == 0:
            lo, n = p - 2, 6
        if seg == 0 and lp == PS - 1:
            lo, n = p - 3, 6
        out.append((cls, lo, n))
    return out


def _na_tables(rpb, SEG, linked):
    PS = SEG // 128
    wins = _na_windows(SEG)
    rows_seg = SEG // 64
    tab = np.full((9, 6, 128, NH, 128), NEG, np.float32)
    done = set()
    ki = np.arange(128)
    for p, (cls, lo, n) in enumerate(wins):
        if cls in done:
            continue
        done.add(cls)
        for qi in range(128):
            gr = 2 * p + qi // 64
            c = qi % 64
            if linked:
                g0, rows = 0, 2 * rows_seg
            else:
                g0, rows = (gr // rows_seg) * rows_seg, rows_seg
            r = gr - g0
            r0 = min(max(r - 4, 0), rows - 8)
            c0 = min(max(c - 8, 0), 64 - 16)
            for w in range(n):
                kr = 2 * (lo + w) + ki // 64 - g0
                kc = ki % 64
                ok = (kr >= r0) & (kr < r0 + 8) & (kc >= c0) & (kc < c0 + 16)
                dr = np.clip(kr - r + 7, 0, 14)
                dc = np.clip(kc - c + 15, 0, 30)
                vals = rpb[:, dr, dc]
                tab[cls, w, :, :, qi] = np.where(ok[None, :], vals, NEG).T
    return tab


def _rope_tables(SEG, linked):
    half = 32
    inv_freq = (10000.0 ** (-np.arange(half, dtype=np.float32) / half)).astype(np.float32)
    pos = np.arange(2 * SEG, dtype=np.float32)
    if not linked:
        pos = np.concatenate([np.arange(SEG, dtype=np.float32)] * 2)
    ang = (pos[:, None] * inv_freq[None, :]).astype(np.float32)
    cos = np.cos(ang).astype(np.float32)
    sin = np.sin(ang).astype(np.float32)
    cf = np.concatenate([cos, cos], 1)
    sf = np.concatenate([-sin, sin], 1)
    tm = np.stack([np.tile(cf, (1, 4)), np.tile(sf, (1, 4))])
    fm = np.stack([np.tile(cf, (1, 2)).T, np.tile(sf, (1, 2)).T])
    return np.ascontiguousarray(fm), np.ascontiguousarray(tm)


def _consts():
    i = np.arange(128)
    c = {}
    c["ident"] = np.eye(128, dtype=np.float32)
    c["ones"] = np.ones((128, 128), np.float32)
    c["tri_le"] = (i[:, None] <= i[None, :]).astype(np.float32)
    c["tri_ge"] = (i[:, None] >= i[None, :]).astype(np.float32)
    c["tri_gt"] = (i[:, None] > i[None, :]).astype(np.float32)
    c["tri_lt"] = (i[:, None] < i[None, :]).astype(np.float32)
    c["m_f_le_p"] = np.where(i[None, :] <= i[:, None], NEG, 0.0).astype(np.float32)
    c["m_p_le_f"] = np.where(i[:, None] <= i[None, :], NEG, 0.0).astype(np.float32)
    c["m_f_lt_p"] = np.where(i[None, :] < i[:, None], NEG, 0.0).astype(np.float32)
    c["m_f_ge_p"] = np.where(i[None, :] >= i[:, None], NEG, 0.0).astype(np.float32)
    c["m_p_ge_f"] = np.where(i[:, None] >= i[None, :], NEG, 0.0).astype(np.float32)
    c["m_f_gt_p"] = np.where(i[None, :] > i[:, None], NEG, 0.0).astype(np.float32)
    c["blk64"] = ((i[:, None] // 64) == (i[None, :] // 64)).astype(np.float32)
    c["dpos"] = np.maximum(i[None, :] - i[:, None], 0).astype(np.float32)
    c["dneg"] = np.maximum(i[:, None] - i[None, :], 0).astype(np.float32)
    c["mge"] = (i[None, :] >= i[:, None]).astype(np.float32)
    c["mle"] = (i[None, :] <= i[:, None]).astype(np.float32)
    c["pidx"] = np.stack([i + 1.0, 128.0 - i, 127.0 - i, i + 0.0], 1).astype(np.float32)
    names = sorted(c)
    return names, np.concatenate([c[n].reshape(128, -1) for n in names], 1), {n: c[n].shape[1] for n in names}


ARENA_DBG = {}
CONST_NAMES, CONST_ARR, CONST_W = _consts()
CONST_OFF = {}
_o = 0
for _n in CONST_NAMES:
    CONST_OFF[_n] = _o
    _o += CONST_W[_n]
CONST_TOT = _o


def _small_pack(inp, l):
    parts = [
        ("gdn_a_log", inp["gdn_a_log"][l].reshape(-1)), ("gdn_dt_bias", inp["gdn_dt_bias"][l].reshape(-1)),
        ("ret_decay", inp["ret_decay"][l].reshape(-1)),
    ]
    return np.concatenate([p[1] for p in parts]).astype(np.float32)[None, :]


def _col_pack(inp, l):
    def ch(v):
        return np.asarray(v, np.float32).reshape(-1, 128).T
    cols = [
        ("conv_a_w", np.concatenate([ch(inp["conv_a_w"][l][j]) for j in range(4)], 1)),
        ("conv_a_b", ch(inp["conv_a_b"][l])),
        ("lru_br", np.concatenate([ch(inp["lru_br"][l][d]) for d in range(2)], 1)),
        ("lru_bi", np.concatenate([ch(inp["lru_bi"][l][d]) for d in range(2)], 1)),
        ("lru_lambda", np.concatenate([ch(inp["lru_lambda"][l][d]) for d in range(2)], 1)),
        ("gdn_conv", np.concatenate([ch(inp["gdn_conv"][l][j]) for j in range(4)], 1)),
        ("gdn_norm", ch(np.tile(inp["gdn_norm"][l], 2))),
        ("ret_norm", ch(np.tile(inp["ret_norm"][l], 2))),
        ("na_qnorm", ch(np.tile(inp["na_qnorm"][l], 2))),
        ("na_knorm", ch(np.tile(inp["na_knorm"][l], 2))),
        ("ffn_conv_w", np.concatenate([ch(inp["ffn_conv_w"][l][j]) for j in range(3)], 1)),
        ("ffn_conv_b", ch(inp["ffn_conv_b"][l])),
        ("norm1", ch(inp["norm1"][l])), ("norm2", ch(inp["norm2"][l])), ("norm3", ch(inp["norm3"][l])),
    ]
    off = {}
    o = 0
    for n, a in cols:
        off[n] = o
        o += a.shape[1]
    return np.ascontiguousarray(np.concatenate([a for _, a in cols], 1)), off, o


def _lru_blockdiag(w):
    out = np.zeros((2, 2, 128, 128), np.float32)
    for d in range(2):
        for h in range(4):
            c, b = divmod(h, 2)
            out[d, c, b * 64:(b + 1) * 64, b * 64:(b + 1) * 64] = w[d, h]
    return out


def build(SEG, dbg=False, stop=None):
    T = 2 * SEG
    NT = T // 128
    NG = T // 512
    PS = SEG // 128
    SP = SEG + 4
    wins = _na_windows(SEG)
    dummy_inp = {k: np.zeros(s, np.float32) for k, s in [
        ("conv_a_w", (1, 4, 256)), ("conv_a_b", (1, 256)), ("lru_br", (1, 2, 256)), ("lru_bi", (1, 2, 256)),
        ("lru_lambda", (1, 2, 256)), ("gdn_conv", (1, 4, 768)), ("gdn_norm", (1, 64)), ("ret_norm", (1, 64)),
        ("na_qnorm", (1, 64)), ("na_knorm", (1, 64)), ("ffn_conv_w", (1, 3, DFF)), ("ffn_conv_b", (1, DFF)),
        ("norm1", (1, D)), ("norm2", (1, D)), ("norm3", (1, D))]}
    _, CO, NCOL = _col_pack(dummy_inp, 0)

    nc = bass.Bass("TRN2", target_bir_lowering=False)
    S = Sched(nc)
    stack = ExitStack()

    def din(name, shape, dt=F32):
        return nc.dram_tensor(name, list(shape), dt, kind="ExternalInput").ap()

    def dscr(name, shape, dt):
        return nc.dram_tensor(name, list(shape), dt, kind="ExternalOutput" if dbg else "Internal").ap()

    x_in = din("x_in", [T, D])
    p_in = din("p_in", [DEPTH, T, PLE])
    flag_in = din("flag", [128, 1])
    consts_in = din("consts", [128, CONST_TOT])
    rope_fm_in = din("rope_fm", [2, 128, T])
    rope_tm_in = din("rope_tm", [2, T, 256])
    natab_in = din("natab", [DEPTH, 9, 6, 128, NH * 128])
    w_in_in = din("w_in", [DEPTH, D, WCOLS])
    w_out_in = din("w_out", [DEPTH, D, D])
    wg_in = din("ffn_wg", [DEPTH, D, DFF])
    wu_in = din("ffn_wu", [DEPTH, D, DFF])
    wd_in = din("ffn_wd", [DEPTH, DFF, D])
    pproj_in = din("ple_proj", [DEPTH, PLE, D])
    pgate_in = din("ple_gate", [DEPTH, D, D])
    lruw_in = din("lru_w", [DEPTH, 2, 2, 2, 128, 128])
    cols_in = din("cols", [DEPTH, 128, NCOL])
    small_in = din("small", [DEPTH, 1, 24])
    y_out = nc.dram_tensor("y_out", [T, D], F32, kind="ExternalOutput").ap()

    XA = dscr("XA", [256, 2, SP], F32)
    GY = dscr("GY", [256, T], BF16)
    QKVB = dscr("QKVB", [768, 2, SP], F32)
    SZ = dscr("SZ", [256, T], BF16)
    BA = dscr("BA", [T, 16], F32)
    QC = dscr("QC", [256, T], BF16)
    KC = dscr("KC", [256, T], BF16)
    KCt = dscr("KCt", [T, 256], BF16)
    VC = dscr("VC", [T, 256], BF16)
    VD1 = dscr("VD1", [T, NH * 65], BF16)
    SGC = dscr("SGC", [256, T], BF16)
    QD = dscr("QD", [256, T], BF16)
    KD = dscr("KD", [256, T], BF16)
    MIX = dscr("MIX", [D, T], BF16)
    QB = dscr("QB", [256, T], BF16)
    KB = dscr("KB", [256, T], BF16)
    KBt = dscr("KBt", [T, 256], BF16)
    VBt = dscr("VBt", [T, 256], BF16)
    OB1 = dscr("OB1", [T, 256], F32)
    SBST = dscr("SBST", [NT, 128, 256], BF16)
    X1 = dscr("X1", [T, D], F32)
    XRES = dscr("XRES", [T, D], F32)

    gran = {}

    def G(name, t0, t1):
        return [gran.setdefault((name, i), Buf("%s%d" % (name, i))) for i in range(t0 // 128, (t1 - 1) // 128 + 1)]

    def GP(name, k):
        return [gran.setdefault((name, "pad", k), Buf("%s_pad%d" % (name, k)))]

    def sb(name, shape, dt):
        t = stack.enter_context(nc.sbuf_tensor(name, list(shape), dt))
        return t, Buf(name)

    arena_state = {"off": 0, "bufs": [], "n": 0}

    def phase_begin():
        arena_state["off"] = 0
        arena_state["bufs"] = []
        arena_state["n"] = 0

    def sbp(name, shape, dt):
        n = 1
        for d_ in shape[1:]:
            n *= d_
        words = (n + 1) // 2 if dt == BF16 else n
        words = (words + 7) // 8 * 8
        o = arena_state["off"]
        arena_state["off"] = o + words
        assert arena_state["off"] <= ARENA_W, (name, arena_state["off"])
        v = arena[:, o:o + words]
        if dt == BF16:
            v = v.bitcast(BF16)
        v = v[:, 0:n]
        if len(shape) == 3:
            v = v.rearrange("p (a b) -> p a b", a=shape[1])
        elif len(shape) == 4:
            v = v.rearrange("p (a b c) -> p a b c", a=shape[1], b=shape[2])
        ARENA_DBG[name] = (o, n, tuple(shape), 'bf16' if dt == BF16 else 'f32')
        b = Buf(name)
        arena_state["bufs"].append(b)
        return v, b

    def phase_alloc_done():
        new = arena_state["bufs"][arena_state["n"]:]
        arena_state["n"] = len(arena_state["bufs"])
        S.dve(lambda: nc.vector.memset(zt[:, 5:6], 0.0), r=[arena_b], w=list(new) + [zt_b])

    def phase_end():
        S.dve(lambda: nc.vector.memset(zt[:, 5:6], 0.0), w=list(arena_state["bufs"]) + [arena_b, zt_b])

    def psum(name, shape, dt):
        t = stack.enter_context(nc.psum_tensor(name, list(shape), dt))
        return t, Buf(name)

    PSF = [psum("psf%d" % i, [128, 512], F32) for i in range(6)]
    PSB = [psum("psb%d" % i, [128, 1024], BF16) for i in range(2)]

    ARENA_W = 17920
    arena, arena_b = sb("arena", [128, ARENA_W], F32)
    cst, cst_b = sb("cst", [128, CONST_TOT], F32)
    cstb, cstb_b = sb("cstb", [128, CONST_TOT], BF16)
    flag, flag_b = sb("flagt", [128, 1], F32)
    cols, cols_b = sb("colst", [128, NCOL], F32)
    small, small_b = sb("smallt", [128, 24], F32)
    der, der_b = sb("der", [128, 64], F32)

    def C(name, bf=False):
        o = CONST_OFF[name]
        return (cstb if bf else cst)[:, o:o + CONST_W[name]]

    def col(name, k=0):
        o = CO[name] + k
        return cols[:, o:o + 1]

    S.load(lambda: nc.sync.dma_start(out=cst[:], in_=consts_in[:, :]), w=[cst_b])
    S.load(lambda: nc.sync.dma_start(out=flag[:], in_=flag_in[:, :]), w=[flag_b])
    S.dve(lambda: nc.vector.tensor_copy(cstb[:], cst[:]), r=[cst_b], w=[cstb_b])
    CB = [cst_b, cstb_b]
    ident_b = C("ident", True)
    ident_f = C("ident")

    zt, zt_b = sb("zt", [128, 8], F32)
    zt16, zt16_b = sb("zt16", [128, 8], BF16)
    S.dve(lambda: nc.vector.memset(zt16[:], 0.0), w=[zt_b])
    S.dve(lambda: nc.vector.memset(zt[:], 0.0), w=[zt_b])
    for c in range(2):
        S.store(lambda c=c: nc.gpsimd.dma_start(out=XA[c * 128:(c + 1) * 128, :, 0:2], in_=zt[:, 0:4].rearrange("p (a b) -> p a b", a=2)), r=[zt_b], w=GP("XA", 0))
        S.store(lambda c=c: nc.gpsimd.dma_start(out=XA[c * 128:(c + 1) * 128, :, SP - 2:SP], in_=zt[:, 0:4].rearrange("p (a b) -> p a b", a=2)), r=[zt_b], w=GP("XA", 1))
    for c in range(6):
        S.store(lambda c=c: nc.gpsimd.dma_start(out=QKVB[c * 128:(c + 1) * 128, :, 0:2], in_=zt[:, 0:4].rearrange("p (a b) -> p a b", a=2)), r=[zt_b], w=GP("QKVB", 0))
        S.store(lambda c=c: nc.gpsimd.dma_start(out=QKVB[c * 128:(c + 1) * 128, :, SP - 2:SP], in_=zt[:, 0:4].rearrange("p (a b) -> p a b", a=2)), r=[zt_b], w=GP("QKVB", 1))

    wbig, wbig_b = sb("wbig", [128, 8 * WCOLS], BF16)
    wstage = [sb("wstage%d" % i, [128, 1024], F32) for i in range(2)]
    xin_t = [sb("xin%d" % i, [128, D], F32) for i in range(2)]
    hn_t, hn_b = sb("hn", [128, D], BF16)
    junk, junk_b = sb("junk", [128, D], BF16)
    hnT = [sb("hnT%d" % i, [128, 8, 512], BF16) for i in range(2)]
    stat, stat_b = sb("stat", [128, 8], F32)
    ev32 = [sb("ev32_%d" % i, [128, 520], F32) for i in range(4)]
    ev16 = [sb("ev16_%d" % i, [128, 1040], BF16) for i in range(3)]
    rope_t = [sb("rope0", [128, 2, 512], F32)] * 2

    state = {"ev32": 0, "ev16": 0}

    def nxt(lst, key):
        i = state[key]
        state[key] = (i + 1) % len(lst)
        return lst[i]

    def load_weight_bf16(dst_ap_fn, src_rows_fn, ncols, scale_col_fn, dst_buf, kchunks):
        for k in range(kchunks):
            for c0 in range(0, ncols, 1024):
                cw = min(1024, ncols - c0)
                st, st_b = wstage[(k + c0 // 1024) % 2]
                S.load(lambda st=st, k=k, c0=c0, cw=cw: nc.sync.dma_start(out=st[:, 0:cw], in_=src_rows_fn(k)[:, c0:c0 + cw]), w=[st_b])
                if scale_col_fn is None:
                    S.act(lambda st=st, k=k, c0=c0, cw=cw: nc.scalar.copy(out=dst_ap_fn(k, c0, cw), in_=st[:, 0:cw]), r=[st_b], w=[dst_buf])
                else:
                    S.dve(lambda st=st, k=k, c0=c0, cw=cw: nc.vector.tensor_scalar(out=dst_ap_fn(k, c0, cw), in0=st[:, 0:cw], scalar1=scale_col_fn(k), scalar2=None, op0=ALU.mult),
                          r=[st_b, cols_b], w=[dst_buf])

    def rmsnorm_tile(x_ap, x_bufs, out_bf, out_buf, sidx):
        S.act(lambda: nc.scalar.activation(out=junk[:], in_=x_ap, func=AF.Square, accum_out=stat[:, sidx:sidx + 1]), r=x_bufs, w=[junk_b, stat_b])
        S.act(lambda: nc.scalar.activation(out=stat[:, sidx + 1:sidx + 2], in_=stat[:, sidx:sidx + 1], func=AF.Sqrt, scale=1.0 / D, bias=der[:, 63:64]), r=[stat_b, der_b], w=[stat_b])
        S.dve(lambda: nc.vector.reciprocal(out=stat[:, sidx + 2:sidx + 3], in_=stat[:, sidx + 1:sidx + 2]), r=[stat_b], w=[stat_b])
        S.act(lambda: nc.scalar.activation(out=out_bf, in_=x_ap, func=AF.Identity, scale=stat[:, sidx + 2:sidx + 3]), r=x_bufs + [stat_b], w=[out_buf])

    def transpose_to(dstT_ap_fn, dst_buf, src_bf, src_buf, nk, psb):
        pt, pt_b = psb
        S.pe(lambda: [nc.tensor.transpose(pt[:, k * 128:(k + 1) * 128], src_bf[:, k * 128:(k + 1) * 128], ident_b) for k in range(nk)][-1],
             r=[src_buf] + CB, w=[pt_b])
        S.dve(lambda: nc.vector.tensor_copy(dstT_ap_fn(), pt[:, 0:nk * 128].rearrange("p (k t) -> p k t", k=nk)), r=[pt_b], w=[dst_buf])

    S.dve(lambda: nc.vector.memset(der[:, 63:64], EPS), w=[der_b])

    for l in range(DEPTH):
        xsrc = x_in if l == 0 else XRES
        xdst = XRES if l == 0 else y_out
        xsrc_name = "x_in" if l == 0 else "XRES"
        xdst_name = "XRES" if l == 0 else "y_out"

        S.load(lambda l=l: nc.sync.dma_start(out=cols[:], in_=cols_in[l]), w=[cols_b])
        S.load(lambda l=l: nc.sync.dma_start(out=small[:], in_=small_in[l].partition_broadcast(128)), w=[small_b])

        wv = wbig[:, 0:8 * WCOLS].rearrange("p (k c) -> p k c", k=8)
        load_weight_bf16(lambda k, c0, cw: wv[:, k, c0:c0 + cw], lambda k, l=l: w_in_in[l, k * 128:(k + 1) * 128, :], WCOLS,
                         lambda k: col("norm1", k), wbig_b, 8)

        def p1_store_fm(dst, name, c, g, src_ap, src_buf):
            S.store(lambda: nc.gpsimd.dma_start(out=dst[c * 128:(c + 1) * 128, g * 512:(g + 1) * 512], in_=src_ap), r=[src_buf], w=G(name, g * 512, g * 512 + 512))

        def p1_store_pad(dst, name, c, g, src_t, src_buf):
            seg, off = divmod(g * 512, SEG)
            S.store(lambda: nc.gpsimd.dma_start(out=dst[c * 128:(c + 1) * 128, seg, 2 + off:2 + off + 512], in_=src_t[:, 0:512]), r=[src_buf], w=G(name, g * 512, g * 512 + 512))
            if seg == 1 and off == 0:
                S.act(lambda: nc.scalar.activation(out=src_t[:, 512:514], in_=src_t[:, 0:2], func=AF.Identity, scale=flag[:, 0:1]), r=[src_buf, flag_b], w=[src_buf])
                S.store(lambda: nc.gpsimd.dma_start(out=dst[c * 128:(c + 1) * 128, 0, SP - 2:SP], in_=src_t[:, 512:514]), r=[src_buf], w=GP(name, 2))
            if seg == 0 and off + 512 == SEG:
                S.act(lambda: nc.scalar.activation(out=src_t[:, 512:514], in_=src_t[:, 510:512], func=AF.Identity, scale=flag[:, 0:1]), r=[src_buf, flag_b], w=[src_buf])
                S.store(lambda: nc.gpsimd.dma_start(out=dst[c * 128:(c + 1) * 128, 1, 0:2], in_=src_t[:, 512:514]), r=[src_buf], w=GP(name, 3))

        S.dve(lambda: nc.vector.tensor_scalar(out=der[:, 0:1], in0=col("na_qnorm"), scalar1=HD ** -0.5, scalar2=None, op0=ALU.mult), r=[cols_b], w=[der_b])

        for g in range(NG):
            hT, hT_b = hnT[g % 2]
            for t in range(4):
                xt, xt_b = xin_t[t % 2]
                tok = g * 512 + t * 128
                S.load(lambda xt=xt, tok=tok: nc.sync.dma_start(out=xt[:], in_=xsrc[tok:tok + 128, :]), r=G(xsrc_name, tok, tok + 128), w=[xt_b])
                rmsnorm_tile(xt[:], [xt_b], hn_t[:], hn_b, 0)
                transpose_to(lambda t=t, hT=hT: hT[:, :, t * 128:(t + 1) * 128], hT_b, hn_t, hn_b, 8, PSB[t % 2])
            rp, rp_b = rope_t[g % 2]
            S.load(lambda rp=rp, g=g: nc.sync.dma_start(out=rp[:], in_=rope_fm_in[:, :, g * 512:(g + 1) * 512].rearrange("a p n -> p a n")), w=[rp_b])
            for c in range(NFM):
                ps, ps_b = PSF[c % 4]
                S.pe(lambda c=c, ps=ps, hT=hT: [nc.tensor.matmul(ps[:], wv[:, k, c * 128:(c + 1) * 128], hT[:, k, :], start=(k == 0), stop=(k == 7)) for k in range(8)][-1],
                     r=[wbig_b, hT_b], w=[ps_b])
                if c in (0, 1):
                    e, e_b = nxt(ev32, "ev32")
                    S.act(lambda e=e, ps=ps: nc.scalar.copy(out=e[:, 0:512], in_=ps[:]), r=[ps_b], w=[e_b])
                    p1_store_pad(XA, "XA", c, g, e, e_b)
                elif c in (2, 3):
                    e, e_b = nxt(ev16, "ev16")
                    S.act(lambda e=e, ps=ps: nc.scalar.activation(out=e[:, 0:512], in_=ps[:], func=AF.Gelu_apprx_tanh), r=[ps_b], w=[e_b])
                    p1_store_fm(GY, "GY", c - 2, g, e[:, 0:512], e_b)
                elif 4 <= c <= 9:
                    e, e_b = nxt(ev32, "ev32")
                    S.act(lambda e=e, ps=ps: nc.scalar.copy(out=e[:, 0:512], in_=ps[:]), r=[ps_b], w=[e_b])
                    p1_store_pad(QKVB, "QKVB", c - 4, g, e, e_b)
                elif c in (10, 11, 20, 21):
                    e, e_b = nxt(ev16, "ev16")
                    S.act(lambda e=e, ps=ps: nc.scalar.activation(out=e[:, 0:512], in_=ps[:], func=AF.Silu), r=[ps_b], w=[e_b])
                    if c < 12:
                        p1_store_fm(SZ, "SZ", c - 10, g, e[:, 0:512], e_b)
                    else:
                        p1_store_fm(SGC, "SGC", c - 20, g, e[:, 0:512], e_b)
                elif c in (12, 13, 16, 17):
                    e, e_b = ev32[(c % 2) * 2]
                    S.dve(lambda e=e, ps=ps, rp=rp: nc.vector.tensor_tensor(out=e[:, 0:512], in0=ps[:], in1=rp[:, 0, :], op=ALU.mult), r=[ps_b, rp_b], w=[e_b])
                elif c in (14, 15, 18, 19):
                    e, e_b = ev32[(c % 2) * 2]
                    e2, e2_b = ev32[(c % 2) * 2 + 1]
                    o16, o16_b = nxt(ev16, "ev16")
                    S.dve(lambda e2=e2, ps=ps, rp=rp: nc.vector.tensor_tensor(out=e2[:, 0:512], in0=ps[:], in1=rp[:, 1, :], op=ALU.mult), r=[ps_b, rp_b], w=[e2_b])
                    if c < 16:
                        S.dve(lambda e=e, e2=e2, o16=o16: nc.vector.tensor_tensor(out=o16[:, 0:512], in0=e[:, 0:512], in1=e2[:, 0:512], op=ALU.add), r=[e_b, e2_b], w=[o16_b])
                        p1_store_fm(QC, "QC", c - 14, g, o16[:, 0:512], o16_b)
                    else:
                        S.dve(lambda e=e, e2=e2: nc.vector.tensor_tensor(out=e[:, 0:512], in0=e[:, 0:512], in1=e2[:, 0:512], op=ALU.add), r=[e_b, e2_b], w=[e_b])
                        S.act(lambda e=e, o16=o16: nc.scalar.mul(out=o16[:, 0:512], in_=e[:, 0:512], mul=HD ** -0.5), r=[e_b], w=[o16_b])
                        p1_store_fm(KC, "KC", c - 18, g, o16[:, 0:512], o16_b)
                else:
                    sq, sq_b = nxt(ev16, "ev16")
                    e, e_b = nxt(ev32, "ev32")
                    o16, o16_b = nxt(ev16, "ev16")
                    ps2, ps2_b = PSF[4 + c % 2]
                    S.act(lambda sq=sq, ps=ps: nc.scalar.activation(out=sq[:, 0:512], in_=ps[:], func=AF.Square), r=[ps_b], w=[sq_b])
                    S.pe(lambda sq=sq, ps2=ps2: nc.tensor.matmul(ps2[:], C("blk64", True), sq[:, 0:512], start=True, stop=True), r=[sq_b] + CB, w=[ps2_b])
                    S.act(lambda e=e, ps2=ps2: nc.scalar.activation(out=e[:, 0:512], in_=ps2[:], func=AF.Sqrt, scale=1.0 / HD, bias=der[:, 63:64]), r=[ps2_b, der_b], w=[e_b])
                    S.dve(lambda e=e: nc.vector.reciprocal(out=e[:, 0:512], in_=e[:, 0:512]), r=[e_b], w=[e_b])
                    gcol = der[:, 0:1] if c < 24 else col("na_knorm")
                    S.dve(lambda e=e, ps=ps, o16=o16, gcol=gcol: nc.vector.scalar_tensor_tensor(out=o16[:, 0:512], in0=ps[:], scalar=gcol, in1=e[:, 0:512], op0=ALU.mult, op1=ALU.mult),
                          r=[ps_b, e_b, der_b, cols_b], w=[o16_b])
                    if c < 24:
                        p1_store_fm(QD, "QD", c - 22, g, o16[:, 0:512], o16_b)
                    else:
                        p1_store_fm(KD, "KD", c - 24, g, o16[:, 0:512], o16_b)
            for t in range(4):
                tok = g * 512 + t * 128
                tb = NFM * 128
                psA, psA_b = PSF[0]
                psB, psB_b = PSF[1]
                psC, psC_b = PSF[2]
                lhs = lambda k, t=t, hT=hT: hT[:, k, t * 128:(t + 1) * 128]
                S.pe(lambda psA=psA, lhs=lhs: [nc.tensor.matmul(psA[:], lhs(k), wv[:, k, tb:tb + 512], start=(k == 0), stop=(k == 7)) for k in range(8)][-1], r=[wbig_b, hT_b], w=[psA_b])
                S.pe(lambda psB=psB, lhs=lhs: [nc.tensor.matmul(psB[:], lhs(k), wv[:, k, tb + 512:tb + 1024], start=(k == 0), stop=(k == 7)) for k in range(8)][-1], r=[wbig_b, hT_b], w=[psB_b])
                S.pe(lambda psC=psC, lhs=lhs: [nc.tensor.matmul(psC[:, 0:16], lhs(k), wv[:, k, tb + 1024:tb + 1040], start=(k == 0), stop=(k == 7)) for k in range(8)][-1], r=[wbig_b, hT_b], w=[psC_b])
                o16, o16_b = nxt(ev16, "ev16")
                S.act(lambda o16=o16, psA=psA: nc.scalar.copy(out=o16[:, 0:256], in_=psA[:, 0:256]), r=[psA_b], w=[o16_b])
                S.dve(lambda o16=o16: nc.vector.memset(o16[:, 512:512 + 260], 1.0), w=[o16_b])
                S.act(lambda o16=o16, psA=psA: nc.scalar.copy(out=o16[:, 512:772].rearrange("p (h e) -> p h e", h=4)[:, :, 0:64], in_=psA[:, 256:512].rearrange("p (h e) -> p h e", h=4)), r=[psA_b], w=[o16_b])
                S.store(lambda o16=o16, tok=tok: nc.gpsimd.dma_start(out=VC[tok:tok + 128, :], in_=o16[:, 0:256]), r=[o16_b], w=G("VC", tok, tok + 128))
                S.store(lambda o16=o16, tok=tok: nc.gpsimd.dma_start(out=VD1[tok:tok + 128, :], in_=o16[:, 512:772]), r=[o16_b], w=G("VD1", tok, tok + 128))
                rt, rt_b = nxt(ev32, "ev32")
                e, e_b = nxt(ev32, "ev32")
                o2, o2_b = nxt(ev16, "ev16")
                S.load(lambda rt=rt, tok=tok: nc.sync.dma_start(out=rt[:, 0:512].rearrange("p (a n) -> p a n", a=2), in_=rope_tm_in[:, tok:tok + 128, :].rearrange("a p n -> p a n")), w=[rt_b])
                S.dve(lambda e=e, psB=psB, rt=rt: nc.vector.tensor_tensor(out=e[:, 0:512], in0=psB[:], in1=rt[:, 0:512], op=ALU.mult), r=[psB_b, rt_b], w=[e_b])
                S.dve(lambda e=e: nc.vector.tensor_tensor(out=e[:, 0:256], in0=e[:, 0:256], in1=e[:, 256:512], op=ALU.add), r=[e_b], w=[e_b])
                S.act(lambda e=e, o2=o2: nc.scalar.mul(out=o2[:, 0:256], in_=e[:, 0:256], mul=HD ** -0.5), r=[e_b], w=[o2_b])
                S.store(lambda o2=o2, tok=tok: nc.gpsimd.dma_start(out=KCt[tok:tok + 128, :], in_=o2[:, 0:256]), r=[o2_b], w=G("KCt", tok, tok + 128))
                e3, e3_b = nxt(ev32, "ev32")
                S.act(lambda e3=e3, psC=psC: nc.scalar.copy(out=e3[:, 0:16], in_=psC[:, 0:16]), r=[psC_b], w=[e3_b])
                S.store(lambda e3=e3, tok=tok: nc.gpsimd.dma_start(out=BA[tok:tok + 128, :], in_=e3[:, 0:16]), r=[e3_b], w=G("BA", tok, tok + 128))
        def head_norm(o, o_b, on, on_b, st, st_b, sq, sq_b, center):
            o3 = o[:].rearrange("p (h e) -> p h e", h=4)
            if center:
                S.dve(lambda: nc.vector.tensor_reduce(out=st[:, 0:4], in_=o3, axis=mybir.AxisListType.X, op=ALU.add), r=[o_b], w=[st_b])
                S.dve(lambda: nc.vector.tensor_scalar(out=st[:, 0:4], in0=st[:, 0:4], scalar1=1.0 / HD, scalar2=None, op0=ALU.mult), r=[st_b], w=[st_b])
                S.dve(lambda: nc.vector.tensor_tensor(out=o3, in0=o3, in1=st[:, 0:4].unsqueeze(2).broadcast_to([128, 4, 64]), op=ALU.subtract), r=[o_b, st_b], w=[o_b])
            S.dve(lambda: nc.vector.tensor_tensor(out=sq[:], in0=o[:], in1=o[:], op=ALU.mult), r=[o_b], w=[sq_b])
            S.dve(lambda: nc.vector.tensor_reduce(out=st[:, 4:8], in_=sq[:].rearrange("p (h e) -> p h e", h=4), axis=mybir.AxisListType.X, op=ALU.add), r=[sq_b], w=[st_b])
            S.act(lambda: nc.scalar.activation(out=st[:, 8:12], in_=st[:, 4:8], func=AF.Sqrt, scale=1.0 / HD, bias=der[:, 63:64]), r=[st_b, der_b], w=[st_b])
            S.dve(lambda: nc.vector.reciprocal(out=st[:, 12:16], in_=st[:, 8:12]), r=[st_b], w=[st_b])
            S.dve(lambda: nc.vector.tensor_tensor(out=on[:].rearrange("p (h e) -> p h e", h=4), in0=o3, in1=st[:, 12:16].unsqueeze(2).broadcast_to([128, 4, 64]), op=ALU.mult), r=[o_b, st_b], w=[on_b])

        if stop == 'p1':
            break
        if True:
            phase_begin()
            P2 = {}
            P2["blk"] = [sbp("blk%d" % i, [128, 520], F32) for i in range(2)]
            P2["xc"] = sbp("xc", [128, 512], F32)
            P2["xcb"] = sbp("xcb", [128, 512], BF16)
            P2["g32"] = [sbp("g32_%d" % i, [128, 512], F32) for i in range(6)]
            P2["lruw"] = sbp("lruw", [128, 8, 128], BF16)
            P2["car"] = sbp("car", [128, 2], F32)
            P2["gy"] = [sbp("gyt%d" % i, [128, 512], BF16) for i in range(2)]
            P2["o16"] = [sbp("p2o16_%d" % i, [128, 512], BF16) for i in range(2)]
            phase_alloc_done()
        hf_all = wbig[:, 0:2 * T].bitcast(F32)
        hf_b = [Buf("hf%d" % i) for i in range(NG)]
        S.dve(lambda: nc.vector.memset(zt[:, 4:5], 0.0), r=[wbig_b], w=hf_b + [zt_b])

        lruw, lruw_b = P2["lruw"]
        for ri in range(2):
            for d in range(2):
                for ck in range(2):
                    st, st_b = wstage[(d + ck) % 2]
                    S.load(lambda st=st, ri=ri, d=d, ck=ck, l=l: nc.sync.dma_start(out=st[:, 0:128], in_=lruw_in[l, ri, d, ck]), w=[st_b])
                    S.act(lambda st=st, ri=ri, d=d, ck=ck: nc.scalar.copy(out=lruw[:, ri * 4 + d * 2 + ck, :], in_=st[:, 0:128]), r=[st_b], w=[lruw_b])
        S.act(lambda: nc.scalar.activation(out=der[:, 1:5], in_=cols[:, CO["lru_lambda"]:CO["lru_lambda"] + 4], func=AF.Exp, scale=-1.0), r=[cols_b], w=[der_b])
        S.act(lambda: nc.scalar.activation(out=der[:, 1:5], in_=der[:, 1:5], func=AF.Ln, bias=1.0), r=[der_b], w=[der_b])
        S.dve(lambda: nc.vector.tensor_scalar(out=der[:, 5:9], in0=der[:, 1:5], scalar1=-16.0, scalar2=None, op0=ALU.mult), r=[der_b], w=[der_b])
        S.dve(lambda: nc.vector.tensor_scalar(out=der[:, 1:5], in0=der[:, 1:5], scalar1=-8.0, scalar2=None, op0=ALU.mult), r=[der_b], w=[der_b])
        car, car_b = P2["car"]
        xc, xc_b = P2["xc"]
        xcb, xcb_b = P2["xcb"]

        def conv4(blk, blk_b, wname, bname, ck, nck, out_t, out_b):
            S.dve(lambda: nc.vector.tensor_scalar(out=out_t[:, 0:512], in0=blk[:, 0:512], scalar1=col(wname, 0 * nck + ck), scalar2=(col(bname, ck) if bname else 0.0), op0=ALU.mult, op1=ALU.add),
                  r=[blk_b, cols_b], w=[out_b])
            for j in range(1, 4):
                S.dve(lambda j=j: nc.vector.scalar_tensor_tensor(out=out_t[:, 0:512], in0=blk[:, j:j + 512], scalar=col(wname, j * nck + ck), in1=out_t[:, 0:512], op0=ALU.mult, op1=ALU.add),
                      r=[blk_b, cols_b, out_b], w=[out_b])

        def load_padded(src, name, ch0, g, blk, blk_b):
            seg, off = divmod(g * 512, SEG)
            rb = G(name, max(g * 512 - 128, 0), min(g * 512 + 640, T)) + GP(name, 0) + GP(name, 1) + GP(name, 2) + GP(name, 3)
            S.load(lambda: nc.sync.dma_start(out=blk[:, 0:515], in_=src[ch0:ch0 + 128, seg, off:off + 515]), r=rb, w=[blk_b])

        for ck in range(2):
            for d in range(2):
                order = range(NG) if d == 0 else range(NG - 1, -1, -1)
                S.dve(lambda d=d: nc.vector.memset(car[:, d:d + 1], 0.0), w=[car_b])
                for g in order:
                    blk, blk_b = P2["blk"][g % 2]
                    load_padded(XA, "XA", ck * 128, g, blk, blk_b)
                    conv4(blk, blk_b, "conv_a_w", "conv_a_b", ck, 2, xc, xc_b)
                    S.act(lambda: nc.scalar.copy(out=xcb[:], in_=xc[:]), r=[xc_b], w=[xcb_b])
                    psr, psr_b = PSF[0]
                    psi, psi_b = PSF[1]
                    S.pe(lambda d=d, ck=ck, psr=psr: nc.tensor.matmul(psr[:], lruw[:, 0 * 4 + d * 2 + ck, :], xcb[:], start=True, stop=True), r=[lruw_b, xcb_b], w=[psr_b])
                    S.pe(lambda d=d, ck=ck, psi=psi: nc.tensor.matmul(psi[:], lruw[:, 1 * 4 + d * 2 + ck, :], xcb[:], start=True, stop=True), r=[lruw_b, xcb_b], w=[psi_b])
                    (rt, rt_b), (it, it_b), (at, at_b), (mt, mt_b), (hb, hb_b), (sm, sm_b) = P2["g32"]
                    S.act(lambda d=d, ck=ck, psr=psr: nc.scalar.activation(out=rt[:], in_=psr[:], func=AF.Sigmoid, bias=col("lru_br", d * 2 + ck)), r=[psr_b, cols_b], w=[rt_b])
                    S.act(lambda d=d, ck=ck, psi=psi: nc.scalar.activation(out=it[:], in_=psi[:], func=AF.Sigmoid, bias=col("lru_bi", d * 2 + ck)), r=[psi_b, cols_b], w=[it_b])
                    S.act(lambda d=d, ck=ck: nc.scalar.activation(out=at[:], in_=rt[:], func=AF.Exp, scale=der[:, 1 + d * 2 + ck:2 + d * 2 + ck]), r=[rt_b, der_b], w=[at_b])
                    S.act(lambda d=d, ck=ck: nc.scalar.activation(out=mt[:], in_=rt[:], func=AF.Exp, scale=der[:, 5 + d * 2 + ck:6 + d * 2 + ck]), r=[rt_b, der_b], w=[mt_b])
                    S.act(lambda: nc.scalar.activation(out=mt[:], in_=mt[:], func=AF.Sqrt, scale=-1.0, bias=1.0), r=[mt_b], w=[mt_b])
                    S.dve(lambda: nc.vector.tensor_tensor(out=mt[:], in0=mt[:], in1=it[:], op=ALU.mult), r=[mt_b, it_b], w=[mt_b])
                    S.dve(lambda: nc.vector.tensor_tensor(out=mt[:], in0=mt[:], in1=xc[:], op=ALU.mult), r=[mt_b, xc_b], w=[mt_b])
                    seg_edge = (g * 512 + 512 == SEG) if d == 0 else (g * 512 == SEG)
                    if d == 0:
                        hdst = hf_all[:, g * 512:(g + 1) * 512]
                        S.dve(lambda hdst=hdst: nc.vector.tensor_tensor_scan(out=hdst, data0=at[:], data1=mt[:], initial=car[:, 0:1], op0=ALU.mult, op1=ALU.add), r=[at_b, mt_b, car_b], w=[hf_b[g]])
                        S.dve(lambda hdst=hdst, seg_edge=seg_edge: nc.vector.tensor_scalar(out=car[:, 0:1], in0=hdst[:, 511:512], scalar1=(flag[:, 0:1] if seg_edge else 1.0), scalar2=None, op0=ALU.mult), r=[hf_b[g], flag_b], w=[car_b])
                    else:
                        S.dve(lambda: nc.vector.tensor_tensor_scan(out=hb[:, ::-1], data0=at[:, ::-1], data1=mt[:, ::-1], initial=car[:, 1:2], op0=ALU.mult, op1=ALU.add), r=[at_b, mt_b, car_b], w=[hb_b])
                        S.dve(lambda seg_edge=seg_edge: nc.vector.tensor_scalar(out=car[:, 1:2], in0=hb[:, 0:1], scalar1=(flag[:, 0:1] if seg_edge else 1.0), scalar2=None, op0=ALU.mult), r=[hb_b, flag_b], w=[car_b])
                        gy, gy_b = P2["gy"][g % 2]
                        o16, o16_b = P2["o16"][g % 2]
                        S.load(lambda gy=gy, ck=ck, g=g: nc.sync.dma_start(out=gy[:], in_=GY[ck * 128:(ck + 1) * 128, g * 512:(g + 1) * 512]), r=G("GY", g * 512, g * 512 + 512), w=[gy_b])
                        S.dve(lambda g=g: nc.vector.tensor_tensor(out=hb[:], in0=hb[:], in1=hf_all[:, g * 512:(g + 1) * 512], op=ALU.add), r=[hb_b, hf_b[g]], w=[hb_b])
                        S.dve(lambda gy=gy, o16=o16: nc.vector.tensor_tensor(out=o16[:], in0=hb[:], in1=gy[:], op=ALU.mult), r=[hb_b, gy_b], w=[o16_b])
                        S.store(lambda o16=o16, ck=ck, g=g: nc.gpsimd.dma_start(out=MIX[ck * 128:(ck + 1) * 128, g * 512:(g + 1) * 512], in_=o16[:]), r=[o16_b], w=G("MIX%d" % ck, g * 512, g * 512 + 512))
        S.dve(lambda: nc.vector.memset(zt[:, 4:5], 0.0), r=hf_b, w=[wbig_b, zt_b])
        phase_end()
        if stop == 'A':
            break
        if True:
            phase_begin()
            PD = {}
            PD["nat0"] = sbp("nat0", [128, 5, 512], F32)
            PD["nate"] = sbp("nate", [128, 6, 512], F32)
            PD["qd"] = [sbp("qd%d" % i, [128, 4, 128], BF16) for i in range(2)]
            PD["kd"] = [sbp("kd%d" % i, [128, 4, 768], BF16) for i in range(2)]
            PD["vd"] = [sbp("vd%d" % i, [128, 6, 260], BF16) for i in range(2)]
            PD["pt"] = [sbp("pt%d" % i, [128, 512], BF16) for i in range(6)]
            PD["od"] = sbp("od", [128, 256], BF16)
            PD["rd"] = sbp("rd", [128, 4], F32)
            PD["mx"] = [sbp("mx%d" % i, [128, 2, 128], BF16) for i in range(2)]
            phase_alloc_done()
        nat0, nat0_b = PD["nat0"]
        nate, nate_b = PD["nate"]
        S.load(lambda l=l: nc.sync.dma_start(out=nat0[:], in_=natab_in[l, 0, 0:5].rearrange("w p e -> p w e")), w=[nat0_b])
        for p, (cls, lo, n) in enumerate(wins):
            tok = p * 128
            qd, qd_b = PD["qd"][p % 2]
            kd, kd_b = PD["kd"][p % 2]
            vd, vd_b = PD["vd"][p % 2]
            S.load(lambda qd=qd, tok=tok: nc.sync.dma_start(out=qd[0:64], in_=QD[:, tok:tok + 128].rearrange("(h d) t -> d h t", h=4)), r=G("QD", tok, tok + 128), w=[qd_b])
            S.load(lambda kd=kd, lo=lo, n=n: nc.sync.dma_start(out=kd[0:64, :, 0:n * 128], in_=KD[:, lo * 128:(lo + n) * 128].rearrange("(h d) t -> d h t", h=4)), r=G("KD", lo * 128, (lo + n) * 128), w=[kd_b])
            S.load(lambda vd=vd, lo=lo, n=n: nc.sync.dma_start(out=vd[:, 0:n, :], in_=VD1[lo * 128:(lo + n) * 128, :].rearrange("(w p) e -> p w e", p=128)), r=G("VD1", lo * 128, (lo + n) * 128), w=[vd_b])
            if cls == 0:
                bias, bias_b = nat0, nat0_b
            else:
                bias, bias_b = nate, nate_b
                S.load(lambda cls=cls, n=n, l=l: nc.sync.dma_start(out=nate[:, 0:n, :], in_=natab_in[l, cls, 0:n].rearrange("w p e -> p w e")), w=[nate_b])
            for w in range(n):
                psS, psS_b = PSF[w % 2]
                pt, pt_b = PD["pt"][w]
                e, e_b = ev32[w % 2]
                S.pe(lambda psS=psS, kd=kd, qd=qd, w=w: [nc.tensor.matmul(psS[:, h * 128:(h + 1) * 128], kd[0:64, h, w * 128:(w + 1) * 128],
                                                                          qd[0:64, h, :], start=True, stop=True) for h in range(4)][-1],
                     r=[kd_b, qd_b], w=[psS_b])
                S.dve(lambda e=e, psS=psS, bias=bias, w=w: nc.vector.tensor_tensor(out=e[:, 0:512], in0=psS[:], in1=bias[:, w, :], op=ALU.add), r=[psS_b, bias_b], w=[e_b])
                S.act(lambda e=e, pt=pt: nc.scalar.activation(out=pt[:], in_=e[:, 0:512], func=AF.Exp), r=[e_b], w=[pt_b])
            psO, psO_b = PSF[2]
            S.pe(lambda psO=psO, vd=vd, n=n: [nc.tensor.matmul(psO[:, h * 65:(h + 1) * 65], PD["pt"][w][0][:, h * 128:(h + 1) * 128], vd[:, w, h * 65:(h + 1) * 65], start=(w == 0), stop=(w == n - 1))
                                             for h in range(4) for w in range(n)][-1],
                 r=[vd_b] + [PD["pt"][w][1] for w in range(n)], w=[psO_b])
            od, od_b = PD["od"]
            rd, rd_b = PD["rd"]
            pv = psO[:, 0:260].rearrange("p (h e) -> p h e", h=4)
            S.dve(lambda pv=pv: nc.vector.reciprocal(out=rd[:].unsqueeze(2), in_=pv[:, :, 64:65]), r=[psO_b], w=[rd_b])
            S.dve(lambda pv=pv: nc.vector.tensor_tensor(out=od[:].rearrange("p (h e) -> p h e", h=4), in0=pv[:, :, 0:64], in1=rd[:].unsqueeze(2).broadcast_to([128, 4, 64]), op=ALU.mult), r=[psO_b, rd_b], w=[od_b])
            mx, mx_b = PD["mx"][p % 2]
            transpose_to(lambda mx=mx: mx[:], mx_b, od, od_b, 2, PSB[p % 2])
            S.store(lambda mx=mx, tok=tok: nc.gpsimd.dma_start(out=MIX[768:1024, tok:tok + 128].rearrange("(c p) t -> p c t", c=2), in_=mx[:]), r=[mx_b], w=G("MIX6", tok, tok + 128) + G("MIX7", tok, tok + 128))
        phase_end()
        if stop == 'D':
            break
        if True:
            phase_begin()
            PC = {}
            PC["lg"] = sbp("lg", [128, 8], F32)
            PC["maskT"] = sbp("maskT", [128, 4, 128], F32)
            PC["tab"] = sbp("ctab", [128, 4, 4, 64], F32)
            PC["t4"] = sbp("ct4", [128, 16], F32)
            PC["gcol"] = sbp("gcolc", [128, 4], F32)
            PC["S"] = [sbp("cS%d" % i, [128, 2, 128], F32) for i in range(2)]
            PC["Sb"] = [sbp("cSb%d" % i, [128, 2, 128], BF16) for i in range(2)]
            PC["q"] = [sbp("cq%d" % i, [128, 2, 128], BF16) for i in range(2)]
            PC["k"] = [sbp("ck%d" % i, [128, 4, 128], BF16) for i in range(2)]
            PC["q4"] = sbp("cq4", [128, 4, 128], BF16)
            PC["kt"] = [sbp("ckt%d" % i, [128, 256], BF16) for i in range(2)]
            PC["v"] = [sbp("cv_%d" % i, [128, 256], BF16) for i in range(2)]
            PC["kdec"] = sbp("ckdec", [128, 256], BF16)
            PC["sbl"] = [sbp("csbl0", [128, 2, 128], BF16)] * 2
            PC["pT"] = sbp("cpT", [128, 512], BF16)
            PC["o"] = sbp("co", [128, 256], F32)
            PC["t1"] = sbp("ct1", [128, 256], F32)
            PC["t2"] = sbp("ct2", [128, 256], F32)
            PC["st"] = sbp("cst4", [128, 24], F32)
            PC["on"] = sbp("con", [128, 256], BF16)
            PC["sg"] = [sbp("csg0", [128, 2, 128], BF16)] * 2
            PC["mx"] = [sbp("cmx%d" % i, [128, 2, 128], BF16) for i in range(2)]
            PC["tmpS"] = sbp("ctmpS", [128, 2, 128], F32)
            phase_alloc_done()
        lg, lg_b = PC["lg"]
        maskT, maskT_b = PC["maskT"]
        ctab, ctab_b = PC["tab"]
        t4, t4_b = PC["t4"]
        gcolc, gcolc_b = PC["gcol"]
        S.act(lambda: nc.scalar.activation(out=lg[:], in_=small[:, 16:24], func=AF.Exp, scale=-float(np.log(2.0))), r=[small_b], w=[lg_b])
        S.act(lambda: nc.scalar.activation(out=lg[:], in_=lg[:], func=AF.Ln, scale=-1.0, bias=1.0), r=[lg_b], w=[lg_b])
        e0, e0_b = ev32[0]
        e1, e1_b = ev32[1]
        for h in range(4):
            S.act(lambda h=h: nc.scalar.activation(out=e0[:, 0:128], in_=C("dpos"), func=AF.Exp, scale=lg[:, h:h + 1]), r=[lg_b] + CB, w=[e0_b])
            S.dve(lambda: nc.vector.tensor_tensor(out=e0[:, 0:128], in0=e0[:, 0:128], in1=C("mge"), op=ALU.mult), r=[e0_b] + CB, w=[e0_b])
            S.act(lambda h=h: nc.scalar.activation(out=e1[:, 0:128], in_=C("dneg"), func=AF.Exp, scale=lg[:, 4 + h:5 + h]), r=[lg_b] + CB, w=[e1_b])
            S.dve(lambda: nc.vector.tensor_tensor(out=e1[:, 0:128], in0=e1[:, 0:128], in1=C("mle"), op=ALU.mult), r=[e1_b] + CB, w=[e1_b])
            S.dve(lambda h=h: nc.vector.tensor_tensor(out=maskT[:, h, :], in0=e0[:, 0:128], in1=e1[:, 0:128], op=ALU.add), r=[e0_b, e1_b], w=[maskT_b])
            for ti, (pi, di) in enumerate([(0, 0), (1, 1), (2, 0), (3, 1)]):
                S.act(lambda h=h, ti=ti, pi=pi, di=di: nc.scalar.activation(out=t4[:, ti * 4 + h:ti * 4 + h + 1], in_=C("pidx")[:, pi:pi + 1], func=AF.Exp, scale=lg[:, di * 4 + h:di * 4 + h + 1]), r=[lg_b] + CB, w=[t4_b])
        for ti in range(4):
            S.dve(lambda ti=ti: nc.vector.tensor_copy(ctab[:, ti], t4[:, ti * 4:ti * 4 + 4].unsqueeze(2).broadcast_to([128, 4, 64])), r=[t4_b], w=[ctab_b])
        for di in range(2):
            for c in range(2):
                for hl in range(2):
                    S.act(lambda di=di, c=c, hl=hl: nc.scalar.activation(out=gcolc[hl * 64:(hl + 1) * 64, di * 2 + c:di * 2 + c + 1], in_=lg[hl * 64:(hl + 1) * 64, di * 4 + 2 * c + hl:di * 4 + 2 * c + hl + 1], func=AF.Exp, scale=128.0), r=[lg_b], w=[gcolc_b])
        blk3 = C("blk64").unsqueeze(1).broadcast_to([128, 2, 128])
        tmpS, tmpS_b = PC["tmpS"]

        def ret_state_update(Sf, Sf_b, Sb16, Sb16_b, kt, kt_b, v, v_b, di):
            kdec, kdec_b = PC["kdec"]
            psK, psK_b = PSF[5]
            S.dve(lambda: nc.vector.tensor_tensor(out=kdec[:].rearrange("p (h e) -> p h e", h=4), in0=kt[:].rearrange("p (h e) -> p h e", h=4), in1=ctab[:, 2 + di], op=ALU.mult), r=[kt_b, ctab_b], w=[kdec_b])
            S.pe(lambda: [nc.tensor.matmul(psK[:, c * 128:(c + 1) * 128], kdec[:, c * 128:(c + 1) * 128], v[:, c * 128:(c + 1) * 128], start=True, stop=True) for c in range(2)][-1], r=[kdec_b, v_b], w=[psK_b])
            S.dve(lambda: nc.vector.tensor_tensor(out=tmpS[:], in0=psK[:, 0:256].rearrange("p (c e) -> p c e", c=2), in1=blk3, op=ALU.mult), r=[psK_b] + CB, w=[tmpS_b])
            for c in range(2):
                S.dve(lambda c=c: nc.vector.scalar_tensor_tensor(out=Sf[:, c, :], in0=Sf[:, c, :], scalar=gcolc[:, di * 2 + c:di * 2 + c + 1], in1=tmpS[:, c, :], op0=ALU.mult, op1=ALU.add), r=[Sf_b, gcolc_b, tmpS_b], w=[Sf_b])
            S.act(lambda: nc.scalar.copy(out=Sb16[:], in_=Sf[:]), r=[Sf_b], w=[Sb16_b])

        (Sb32, Sb32_b), (Sf32, Sf32_b) = PC["S"]
        (Sbb, Sbb_b), (Sfb, Sfb_b) = PC["Sb"]
        for (s32, s32_b, s16, s16_b) in ((Sb32, Sb32_b, Sbb, Sbb_b), (Sf32, Sf32_b, Sfb, Sfb_b)):
            S.dve(lambda s32=s32: nc.vector.memset(s32[:], 0.0), w=[s32_b])
            S.dve(lambda s16=s16: nc.vector.memset(s16[:], 0.0), w=[s16_b])
        for n in range(NT - 1, -1, -1):
            tok = n * 128
            kt, kt_b = PC["kt"][n % 2]
            v, v_b = PC["v"][n % 2]
            S.load(lambda kt=kt, tok=tok: nc.sync.dma_start(out=kt[:], in_=KCt[tok:tok + 128, :]), r=G("KCt", tok, tok + 128), w=[kt_b])
            S.load(lambda v=v, tok=tok: nc.sync.dma_start(out=v[:], in_=VC[tok:tok + 128, :]), r=G("VC", tok, tok + 128), w=[v_b])
            if n == PS - 1:
                S.dve(lambda: nc.vector.tensor_scalar(out=Sb32[:], in0=Sb32[:], scalar1=flag[:, 0:1], scalar2=None, op0=ALU.mult), r=[Sb32_b, flag_b], w=[Sb32_b])
                S.act(lambda: nc.scalar.copy(out=Sbb[:], in_=Sb32[:]), r=[Sb32_b], w=[Sbb_b])
            S.store(lambda n=n: nc.gpsimd.dma_start(out=SBST[n].rearrange("p (c e) -> p c e", c=2), in_=Sbb[:]), r=[Sbb_b], w=G("SBST", tok, tok + 128))
            ret_state_update(Sb32, Sb32_b, Sbb, Sbb_b, kt, kt_b, v, v_b, 1)
        for n in range(NT):
            tok = n * 128
            q, q_b = PC["q"][n % 2]
            k, k_b = PC["k"][n % 2]
            kt, kt_b = PC["kt"][n % 2]
            v, v_b = PC["v"][n % 2]
            sbl, sbl_b = PC["sbl"][n % 2]
            sg, sg_b = PC["sg"][n % 2]
            S.load(lambda q=q, tok=tok: nc.sync.dma_start(out=q[:], in_=QC[:, tok:tok + 128].rearrange("(c p) t -> p c t", c=2)), r=G("QC", tok, tok + 128), w=[q_b])
            q4, q4_b = PC["q4"]
            S.load(lambda q4=q4, tok=tok: nc.sync.dma_start(out=q4[0:64], in_=QC[:, tok:tok + 128].rearrange("(h d) t -> d h t", h=4)), r=G("QC", tok, tok + 128), w=[q4_b])
            S.load(lambda k=k, tok=tok: nc.sync.dma_start(out=k[0:64], in_=KC[:, tok:tok + 128].rearrange("(h d) t -> d h t", h=4)), r=G("KC", tok, tok + 128), w=[k_b])
            S.load(lambda kt=kt, tok=tok: nc.sync.dma_start(out=kt[:], in_=KCt[tok:tok + 128, :]), r=G("KCt", tok, tok + 128), w=[kt_b])
            S.load(lambda v=v, tok=tok: nc.sync.dma_start(out=v[:], in_=VC[tok:tok + 128, :]), r=G("VC", tok, tok + 128), w=[v_b])
            S.load(lambda sbl=sbl, n=n: nc.sync.dma_start(out=sbl[:], in_=SBST[n].rearrange("p (c e) -> p c e", c=2)), r=G("SBST", tok, tok + 128), w=[sbl_b])
            S.load(lambda sg=sg, tok=tok: nc.sync.dma_start(out=sg[:], in_=SGC[:, tok:tok + 128].rearrange("(c p) t -> p c t", c=2)), r=G("SGC", tok, tok + 128), w=[sg_b])
            if n == PS:
                S.dve(lambda: nc.vector.tensor_scalar(out=Sf32[:], in0=Sf32[:], scalar1=flag[:, 0:1], scalar2=None, op0=ALU.mult), r=[Sf32_b, flag_b], w=[Sf32_b])
                S.act(lambda: nc.scalar.copy(out=Sfb[:], in_=Sf32[:]), r=[Sf32_b], w=[Sfb_b])
            psS, psS_b = PSF[0]
            psO, psO_b = PSF[1]
            psXf, psXf_b = PSF[2]
            psXb, psXb_b = PSF[3]
            pT, pT_b = PC["pT"]
            S.pe(lambda k=k, q4=q4: [nc.tensor.matmul(psS[:, h * 128:(h + 1) * 128], k[0:64, h, :], q4[0:64, h, :], start=True, stop=True) for h in range(4)][-1], r=[k_b, q4_b], w=[psS_b])
            S.dve(lambda: nc.vector.tensor_tensor(out=pT[:], in0=psS[:], in1=maskT[:].rearrange("p h n -> p (h n)"), op=ALU.mult), r=[psS_b, maskT_b], w=[pT_b])
            S.pe(lambda v=v: [nc.tensor.matmul(psO[:, h * 64:(h + 1) * 64], pT[:, h * 128:(h + 1) * 128], v[:, h * 64:(h + 1) * 64], start=True, stop=True) for h in range(4)][-1], r=[pT_b, v_b], w=[psO_b])
            S.pe(lambda q=q: [nc.tensor.matmul(psXf[:, c * 128:(c + 1) * 128], q[:, c, :], Sfb[:, c, :], start=True, stop=True) for c in range(2)][-1], r=[q_b, Sfb_b], w=[psXf_b])
            S.pe(lambda q=q, sbl=sbl: [nc.tensor.matmul(psXb[:, c * 128:(c + 1) * 128], q[:, c, :], sbl[:, c, :], start=True, stop=True) for c in range(2)][-1], r=[q_b, sbl_b], w=[psXb_b])
            o, o_b = PC["o"]
            t1, t1_b = PC["t1"]
            t2, t2_b = PC["t2"]
            st, st_b = PC["st"]
            on, on_b = PC["on"]
            S.dve(lambda: nc.vector.tensor_tensor(out=t1[:], in0=psXf[:, 0:256], in1=ctab[:, 0].rearrange("p h e -> p (h e)"), op=ALU.mult), r=[psXf_b, ctab_b], w=[t1_b])
            S.dve(lambda: nc.vector.tensor_tensor(out=t2[:], in0=psXb[:, 0:256], in1=ctab[:, 1].rearrange("p h e -> p (h e)"), op=ALU.mult), r=[psXb_b, ctab_b], w=[t2_b])
            S.dve(lambda: nc.vector.tensor_tensor(out=t1[:], in0=t1[:], in1=t2[:], op=ALU.add), r=[t1_b, t2_b], w=[t1_b])
            S.dve(lambda: nc.vector.tensor_tensor(out=o[:], in0=psO[:, 0:256], in1=t1[:], op=ALU.add), r=[psO_b, t1_b], w=[o_b])
            head_norm(o, o_b, on, on_b, st, st_b, t2, t2_b, True)
            mx, mx_b = PC["mx"][n % 2]
            pt_, pt_b_ = PSB[n % 2]
            S.pe(lambda pt_=pt_: [nc.tensor.transpose(pt_[:, c * 128:(c + 1) * 128], on[:, c * 128:(c + 1) * 128], ident_b) for c in range(2)][-1], r=[on_b] + CB, w=[pt_b_])
            for c in range(2):
                S.dve(lambda c=c, pt_=pt_, mx=mx, sg=sg: nc.vector.scalar_tensor_tensor(out=mx[:, c, :], in0=pt_[:, c * 128:(c + 1) * 128], scalar=col("ret_norm"), in1=sg[:, c, :], op0=ALU.mult, op1=ALU.mult), r=[pt_b_, cols_b, sg_b], w=[mx_b])
            S.store(lambda mx=mx, tok=tok: nc.gpsimd.dma_start(out=MIX[512:768, tok:tok + 128].rearrange("(c p) t -> p c t", c=2), in_=mx[:]), r=[mx_b], w=G("MIX4", tok, tok + 128) + G("MIX5", tok, tok + 128))
            ret_state_update(Sf32, Sf32_b, Sfb, Sfb_b, kt, kt_b, v, v_b, 0)
        if stop == 'C':
            break
        if True:
            PB = {}
            P2 = {}
            P2["blk"] = [sbp("bblk%d" % i, [128, 520], F32) for i in range(2)]
            PB["xs"] = [sbp("bxs0", [128, 512], F32)] * 2
            PB["qn"] = [sbp("bqn%d" % i, [128, 512], BF16) for i in range(2)]
            PB["tk"] = [sbp("btk%d" % i, [128, 4, 256], BF16) for i in range(2)]
            PB["q"] = [sbp("bq%d" % i, [128, 2, 128], BF16) for i in range(2)]
            PB["k"] = [sbp("bk%d" % i, [128, 4, 128], BF16) for i in range(2)]
            PB["q4"] = sbp("bq4", [128, 4, 128], BF16)
            PB["kt"] = [sbp("bkt%d" % i, [128, 256], BF16) for i in range(2)]
            PB["vt"] = [sbp("bvt%d" % i, [128, 256], BF16) for i in range(2)]
            PB["ba"] = [sbp("bba0", [128, 16], F32)] * 2
            PB["sc"] = sbp("bsc", [128, 64], F32)
            PB["W1"] = sbp("bW1", [128, 4, 128], F32)
            PB["R12"] = sbp("bR12", [128, 4, 128], F32)
            PB["E"] = [sbp("bE%d" % i, [128, 512], F32) for i in range(3)]
            PB["TT"] = [sbp("bTT%d" % i, [128, 512], F32) for i in range(2)]
            PB["Tm"] = [sbp("bTm%d" % i, [128, 512], F32) for i in range(2)]
            PB["AT"] = sbp("bAT", [128, 512], BF16)
            PB["y32"] = sbp("by32", [128, 4, 128], F32)
            PB["wk"] = sbp("bwk", [128, 256], BF16)
            PB["wkT"] = sbp("bwkT", [128, 2, 128], BF16)
            PB["vn"] = sbp("bvn", [128, 256], BF16)
            PB["kg"] = sbp("bkg", [128, 256], BF16)
            PB["S"] = sbp("bS", [128, 2, 128], F32)
            PB["Sb"] = sbp("bSb", [128, 2, 128], BF16)
            PB["o"] = sbp("bo", [128, 256], F32)
            PB["ob1"] = [sbp("bob1_0", [128, 256], F32)] * 2
            PB["sz"] = [sbp("bsz0", [128, 2, 128], BF16)] * 2
            PB["negones"] = sbp("bnegones", [128, 128], F32)
            PB["nega"] = sbp("bnega", [128, 8], F32)
            phase_alloc_done()
        negones, negones_b = PB["negones"]
        nega, nega_b = PB["nega"]
        S.dve(lambda: nc.vector.memset(negones[:], -1.0), w=[negones_b])
        S.act(lambda: nc.scalar.activation(out=nega[:], in_=small[:, 0:8], func=AF.Exp), r=[small_b], w=[nega_b])
        S.dve(lambda: nc.vector.tensor_scalar(out=nega[:], in0=nega[:], scalar1=-1.0, scalar2=None, op0=ALU.mult), r=[nega_b], w=[nega_b])
        for g in range(NG):
            tk, tk_b = PB["tk"][g % 2]
            for cc in range(6):
                blk, blk_b = P2["blk"][cc % 2]
                xs, xs_b = PB["xs"][cc % 2]
                load_padded(QKVB, "QKVB", cc * 128, g, blk, blk_b)
                conv4(blk, blk_b, "gdn_conv", None, cc, 6, xs, xs_b)
                S.act(lambda xs=xs: nc.scalar.activation(out=xs[:], in_=xs[:], func=AF.Silu), r=[xs_b], w=[xs_b])
                qn, qn_b = PB["qn"][cc % 2]
                if cc < 4:
                    sq, sq_b = nxt(ev16, "ev16")
                    e, e_b = nxt(ev32, "ev32")
                    ps2, ps2_b = PSF[4 + cc % 2]
                    S.act(lambda sq=sq, xs=xs: nc.scalar.activation(out=sq[:, 0:512], in_=xs[:], func=AF.Square), r=[xs_b], w=[sq_b])
                    S.pe(lambda sq=sq, ps2=ps2: nc.tensor.matmul(ps2[:], C("blk64", True), sq[:, 0:512], start=True, stop=True), r=[sq_b] + CB, w=[ps2_b])
                    S.act(lambda e=e, ps2=ps2: nc.scalar.activation(out=e[:, 0:512], in_=ps2[:], func=AF.Sqrt, bias=der[:, 63:64]), r=[ps2_b, der_b], w=[e_b])
                    S.dve(lambda e=e: nc.vector.reciprocal(out=e[:, 0:512], in_=e[:, 0:512]), r=[e_b], w=[e_b])
                    S.dve(lambda e=e, xs=xs, qn=qn, cc=cc: nc.vector.scalar_tensor_tensor(out=qn[:], in0=xs[:], scalar=(HD ** -0.5 if cc < 2 else 1.0), in1=e[:, 0:512], op0=ALU.mult, op1=ALU.mult), r=[xs_b, e_b], w=[qn_b])
                    dstT, dname = (QB, "QB") if cc < 2 else (KB, "KB")
                    S.store(lambda qn=qn, dstT=dstT, cc=cc, g=g: nc.gpsimd.dma_start(out=dstT[(cc % 2) * 128:(cc % 2 + 1) * 128, g * 512:(g + 1) * 512], in_=qn[:]), r=[qn_b], w=G(dname, g * 512, g * 512 + 512))
                else:
                    S.act(lambda xs=xs, qn=qn: nc.scalar.copy(out=qn[:], in_=xs[:]), r=[xs_b], w=[qn_b])
                if cc >= 2:
                    pt_, pt_b_ = PSB[cc % 2]
                    S.pe(lambda pt_=pt_, qn=qn: [nc.tensor.transpose(pt_[:, t * 128:(t + 1) * 128], qn[:, t * 128:(t + 1) * 128], ident_b) for t in range(4)][-1], r=[qn_b] + CB, w=[pt_b_])
                    S.dve(lambda pt_=pt_, tk=tk, cc=cc: nc.vector.tensor_copy(tk[:, :, (cc - 2) % 2 * 128:((cc - 2) % 2 + 1) * 128], pt_[:, 0:512].rearrange("p (t c) -> p t c", t=4)), r=[pt_b_], w=[tk_b])
                    if cc in (3, 5):
                        dstt, dn = (KBt, "KBt") if cc == 3 else (VBt, "VBt")
                        S.store(lambda tk=tk, dstt=dstt, g=g: nc.gpsimd.dma_start(out=dstt[g * 512:(g + 1) * 512, :].rearrange("(t p) c -> p t c", p=128), in_=tk[:]), r=[tk_b], w=G(dn, g * 512, g * 512 + 512))
                        if cc == 3:
                            tk, tk_b = PB["tk"][(g + 1) % 2]
        Sst, Sst_b = PB["S"]
        Sbf, Sbf_b = PB["Sb"]
        sc, sc_b = PB["sc"]
        W1, W1_b = PB["W1"]
        R12, R12_b = PB["R12"]
        y32, y32_b = PB["y32"]
        for di in range(2):
            tri = C("tri_le") if di == 0 else C("tri_ge")
            trix = C("tri_gt") if di == 0 else C("tri_lt")
            mk1 = C("m_f_le_p") if di == 0 else C("m_f_ge_p")
            mk2 = C("m_p_le_f") if di == 0 else C("m_p_ge_f")
            mk3 = C("m_f_lt_p") if di == 0 else C("m_f_gt_p")
            S.dve(lambda: nc.vector.memset(Sst[:], 0.0), w=[Sst_b])
            S.dve(lambda: nc.vector.memset(Sbf[:], 0.0), w=[Sbf_b])
            for n in (range(NT) if di == 0 else range(NT - 1, -1, -1)):
                tok = n * 128
                q, q_b = PB["q"][n % 2]
                k, k_b = PB["k"][n % 2]
                kt, kt_b = PB["kt"][n % 2]
                vt, vt_b = PB["vt"][n % 2]
                ba, ba_b = PB["ba"][n % 2]
                S.load(lambda q=q, tok=tok: nc.sync.dma_start(out=q[:], in_=QB[:, tok:tok + 128].rearrange("(c p) t -> p c t", c=2)), r=G("QB", tok, tok + 128), w=[q_b])
                q4, q4_b = PB["q4"]
                S.load(lambda q4=q4, tok=tok: nc.sync.dma_start(out=q4[0:64], in_=QB[:, tok:tok + 128].rearrange("(h d) t -> d h t", h=4)), r=G("QB", tok, tok + 128), w=[q4_b])
                S.load(lambda k=k, tok=tok: nc.sync.dma_start(out=k[0:64], in_=KB[:, tok:tok + 128].rearrange("(h d) t -> d h t", h=4)), r=G("KB", tok, tok + 128), w=[k_b])
                S.load(lambda kt=kt, tok=tok: nc.sync.dma_start(out=kt[:], in_=KBt[tok:tok + 128, :]), r=G("KBt", tok, tok + 128), w=[kt_b])
                S.load(lambda vt=vt, tok=tok: nc.sync.dma_start(out=vt[:], in_=VBt[tok:tok + 128, :]), r=G("VBt", tok, tok + 128), w=[vt_b])
                S.load(lambda ba=ba, tok=tok: nc.sync.dma_start(out=ba[:], in_=BA[tok:tok + 128, :]), r=G("BA", tok, tok + 128), w=[ba_b])
                if (di == 0 and n == PS) or (di == 1 and n == PS - 1):
                    S.dve(lambda: nc.vector.tensor_scalar(out=Sst[:], in0=Sst[:], scalar1=flag[:, 0:1], scalar2=None, op0=ALU.mult), r=[Sst_b, flag_b], w=[Sst_b])
                    S.act(lambda: nc.scalar.copy(out=Sbf[:], in_=Sst[:]), r=[Sst_b], w=[Sbf_b])
                S.act(lambda ba=ba, di=di: nc.scalar.activation(out=sc[:, 0:4], in_=ba[:, di * 4:di * 4 + 4], func=AF.Exp, scale=-1.0), r=[ba_b], w=[sc_b])
                S.act(lambda: nc.scalar.activation(out=sc[:, 0:4], in_=sc[:, 0:4], func=AF.Ln, bias=1.0), r=[sc_b], w=[sc_b])
                S.dve(lambda: nc.vector.tensor_scalar(out=sc[:, 0:4], in0=sc[:, 0:4], scalar1=-1.0, scalar2=None, op0=ALU.mult), r=[sc_b], w=[sc_b])
                S.act(lambda: nc.scalar.activation(out=sc[:, 4:8], in_=sc[:, 0:4], func=AF.Exp), r=[sc_b], w=[sc_b])
                S.dve(lambda ba=ba, di=di: nc.vector.tensor_tensor(out=sc[:, 12:16], in0=ba[:, 8 + di * 4:12 + di * 4], in1=small[:, 8 + di * 4:12 + di * 4], op=ALU.add), r=[ba_b, small_b], w=[sc_b])
                S.act(lambda: nc.scalar.activation(out=sc[:, 12:16], in_=sc[:, 12:16], func=AF.Exp), r=[sc_b], w=[sc_b])
                S.act(lambda: nc.scalar.activation(out=sc[:, 12:16], in_=sc[:, 12:16], func=AF.Ln, bias=1.0), r=[sc_b], w=[sc_b])
                S.dve(lambda di=di: nc.vector.tensor_tensor(out=sc[:, 8:12], in0=sc[:, 12:16], in1=nega[:, di * 4:di * 4 + 4], op=ALU.mult), r=[sc_b, nega_b], w=[sc_b])
                for h in range(4):
                    S.dve(lambda h=h, tri=tri: nc.vector.tensor_scalar(out=W1[:, h, :], in0=tri, scalar1=sc[:, 8 + h:9 + h], scalar2=None, op0=ALU.mult), r=[sc_b] + CB, w=[W1_b])
                    S.dve(lambda h=h: nc.vector.scalar_tensor_tensor(out=R12[:, h, :], in0=ident_f, scalar=sc[:, h:h + 1], in1=W1[:, h, :], op0=ALU.mult, op1=ALU.add), r=[sc_b, W1_b] + CB, w=[R12_b])
                ps1, ps1_b = PSF[0]
                ps2, ps2_b = PSF[1]
                ps3, ps3_b = PSF[2]
                psK, psK_b = PSF[3]
                psQ, psQ_b = PSF[4]
                psc, psc_b = PSF[5]
                ones_f = C("ones")
                S.pe(lambda mk1=mk1: [x for h in range(4) for x in (
                    nc.tensor.matmul(ps1[:, h * 128:(h + 1) * 128], ones_f, R12[:, h, :], start=True, stop=False),
                    nc.tensor.matmul(ps1[:, h * 128:(h + 1) * 128], W1[:, h, :], negones[:], start=False, stop=False),
                    nc.tensor.matmul(ps1[:, h * 128:(h + 1) * 128], ident_f, mk1, start=False, stop=True))][-1], r=[R12_b, W1_b, negones_b] + CB, w=[ps1_b])
                S.pe(lambda mk2=mk2: [x for h in range(4) for x in (
                    nc.tensor.matmul(ps2[:, h * 128:(h + 1) * 128], R12[:, h, :], ones_f, start=True, stop=False),
                    nc.tensor.matmul(ps2[:, h * 128:(h + 1) * 128], negones[:], W1[:, h, :], start=False, stop=False),
                    nc.tensor.matmul(ps2[:, h * 128:(h + 1) * 128], ident_f, mk2, start=False, stop=True))][-1], r=[R12_b, W1_b, negones_b] + CB, w=[ps2_b])
                S.pe(lambda mk3=mk3: [x for h in range(4) for x in (
                    nc.tensor.matmul(ps3[:, h * 128:(h + 1) * 128], ones_f, W1[:, h, :], start=True, stop=False),
                    nc.tensor.matmul(ps3[:, h * 128:(h + 1) * 128], W1[:, h, :], negones[:], start=False, stop=False),
                    nc.tensor.matmul(ps3[:, h * 128:(h + 1) * 128], ident_f, mk3, start=False, stop=True))][-1], r=[W1_b, negones_b] + CB, w=[ps3_b])
                S.pe(lambda k=k: [nc.tensor.matmul(psK[:, h * 128:(h + 1) * 128], k[0:64, h, :], k[0:64, h, :], start=True, stop=True) for h in range(4)][-1], r=[k_b], w=[psK_b])
                S.pe(lambda k=k, q4=q4: [nc.tensor.matmul(psQ[:, h * 128:(h + 1) * 128], k[0:64, h, :], q4[0:64, h, :], start=True, stop=True) for h in range(4)][-1], r=[k_b, q4_b], w=[psQ_b])
                S.pe(lambda tri=tri, trix=trix: [nc.tensor.matmul(psc[:, 0:4], tri, sc[:, 8:12], start=True, stop=True), nc.tensor.matmul(psc[:, 4:8], trix, sc[:, 8:12], start=True, stop=True),
                                                 nc.tensor.matmul(psc[:, 8:12], ones_f, sc[:, 8:12], start=True, stop=True)][-1], r=[sc_b] + CB, w=[psc_b])
                (E1, E1_b), (E2, E2_b), (E3, E3_b) = PB["E"]
                S.act(lambda: nc.scalar.activation(out=E1[:], in_=ps1[:], func=AF.Exp), r=[ps1_b], w=[E1_b])
                S.act(lambda: nc.scalar.activation(out=E2[:], in_=ps2[:], func=AF.Exp), r=[ps2_b], w=[E2_b])
                S.act(lambda: nc.scalar.activation(out=E3[:], in_=ps3[:], func=AF.Exp), r=[ps3_b], w=[E3_b])
                S.act(lambda: nc.scalar.activation(out=sc[:, 16:28], in_=psc[:, 0:12], func=AF.Exp), r=[psc_b], w=[sc_b])
                S.dve(lambda: nc.vector.tensor_tensor(out=sc[:, 28:32], in0=sc[:, 4:8], in1=sc[:, 16:20], op=ALU.mult), r=[sc_b], w=[sc_b])
                for hl in range(2):
                    S.dve(lambda hl=hl: nc.vector.tensor_copy(sc[hl * 64:(hl + 1) * 64, 32:34], sc[hl * 64:(hl + 1) * 64, 24 + hl:28:2]), r=[sc_b], w=[sc_b])
                TT, TT_b = PB["TT"][0]
                Tm, Tm_b = PB["Tm"][0]
                AT, AT_b = PB["AT"]
                S.dve(lambda TT=TT: nc.vector.scalar_tensor_tensor(out=TT[:], in0=psK[:], scalar=-1.0, in1=E1[:], op0=ALU.mult, op1=ALU.mult), r=[psK_b, E1_b], w=[TT_b])
                S.dve(lambda Tm=Tm: nc.vector.scalar_tensor_tensor(out=Tm[:], in0=psK[:], scalar=-1.0, in1=E2[:], op0=ALU.mult, op1=ALU.mult), r=[psK_b, E2_b], w=[Tm_b])
                S.dve(lambda: nc.vector.tensor_tensor(out=AT[:], in0=psQ[:], in1=E3[:], op=ALU.mult), r=[psQ_b, E3_b], w=[AT_b])
                S.dve(lambda vt=vt: nc.vector.tensor_tensor(out=y32[:, :, 0:64], in0=vt[:].rearrange("p (h e) -> p h e", h=4), in1=sc[:, 4:8].unsqueeze(2).broadcast_to([128, 4, 64]), op=ALU.mult), r=[vt_b, sc_b], w=[y32_b])
                S.dve(lambda kt=kt: nc.vector.tensor_tensor(out=y32[:, :, 64:128], in0=kt[:].rearrange("p (h e) -> p h e", h=4), in1=sc[:, 28:32].unsqueeze(2).broadcast_to([128, 4, 64]), op=ALU.mult), r=[kt_b, sc_b], w=[y32_b])
                for lv in range(7):
                    psY, psY_b = PSF[0]
                    S.pe(lambda TT=TT: [nc.tensor.matmul(psY[:, h * 128:(h + 1) * 128], TT[:, h * 128:(h + 1) * 128], y32[:, h, :], start=True, stop=True) for h in range(4)][-1], r=[TT_b, y32_b], w=[psY_b])
                    S.dve(lambda: nc.vector.tensor_tensor(out=y32[:].rearrange("p h e -> p (h e)"), in0=y32[:].rearrange("p h e -> p (h e)"), in1=psY[:], op=ALU.add), r=[y32_b, psY_b], w=[y32_b])
                    if lv < 6:
                        psP, psP_b = PSF[1]
                        psPT, psPT_b = PSF[2]
                        TTn, TTn_b = PB["TT"][(lv + 1) % 2]
                        Tmn, Tmn_b = PB["Tm"][(lv + 1) % 2]
                        S.pe(lambda TT=TT, Tm=Tm: [nc.tensor.matmul(psP[:, h * 128:(h + 1) * 128], TT[:, h * 128:(h + 1) * 128], Tm[:, h * 128:(h + 1) * 128], start=True, stop=True) for h in range(4)][-1], r=[TT_b, Tm_b], w=[psP_b])
                        S.pe(lambda TT=TT, Tm=Tm: [nc.tensor.matmul(psPT[:, h * 128:(h + 1) * 128], Tm[:, h * 128:(h + 1) * 128], TT[:, h * 128:(h + 1) * 128], start=True, stop=True) for h in range(4)][-1], r=[TT_b, Tm_b], w=[psPT_b])
                        S.act(lambda Tmn=Tmn: nc.scalar.copy(out=Tmn[:], in_=psP[:]), r=[psP_b], w=[Tmn_b])
                        S.dve(lambda TTn=TTn: nc.vector.tensor_copy(TTn[:], psPT[:]), r=[psPT_b], w=[TTn_b])
                        TT, TT_b, Tm, Tm_b = TTn, TTn_b, Tmn, Tmn_b
                wk, wk_b = PB["wk"]
                wkT, wkT_b = PB["wkT"]
                vn, vn_b = PB["vn"]
                kg, kg_b = PB["kg"]
                o, o_b = PB["o"]
                S.dve(lambda: nc.vector.tensor_copy(wk[:].rearrange("p (h e) -> p h e", h=4), y32[:, :, 64:128]), r=[y32_b], w=[wk_b])
                transpose_to(lambda: wkT[:], wkT_b, wk, wk_b, 2, PSB[n % 2])
                psV, psV_b = PSF[3]
                psO, psO_b = PSF[4]
                psO2, psO2_b = PSF[5]
                S.pe(lambda: [nc.tensor.matmul(psV[:, c * 128:(c + 1) * 128], wkT[:, c, :], Sbf[:, c, :], start=True, stop=True) for c in range(2)][-1], r=[wkT_b, Sbf_b], w=[psV_b])
                S.dve(lambda: nc.vector.tensor_tensor(out=vn[:].rearrange("p (h e) -> p h e", h=4), in0=y32[:, :, 0:64], in1=psV[:, 0:256].rearrange("p (h e) -> p h e", h=4), op=ALU.subtract), r=[y32_b, psV_b], w=[vn_b])
                S.pe(lambda q=q: [nc.tensor.matmul(psO[:, c * 128:(c + 1) * 128], q[:, c, :], Sbf[:, c, :], start=True, stop=True) for c in range(2)][-1], r=[q_b, Sbf_b], w=[psO_b])
                S.pe(lambda: [nc.tensor.matmul(psO2[:, h * 64:(h + 1) * 64], AT[:, h * 128:(h + 1) * 128], vn[:, h * 64:(h + 1) * 64], start=True, stop=True) for h in range(4)][-1], r=[AT_b, vn_b], w=[psO2_b])
                S.dve(lambda: nc.vector.tensor_tensor(out=o[:].rearrange("p (h e) -> p h e", h=4), in0=psO[:, 0:256].rearrange("p (h e) -> p h e", h=4), in1=sc[:, 16:20].unsqueeze(2).broadcast_to([128, 4, 64]), op=ALU.mult), r=[psO_b, sc_b], w=[o_b])
                S.dve(lambda: nc.vector.tensor_tensor(out=o[:], in0=o[:], in1=psO2[:, 0:256], op=ALU.add), r=[o_b, psO2_b], w=[o_b])
                psS2, psS2_b = PSF[3]
                S.dve(lambda kt=kt: nc.vector.tensor_tensor(out=kg[:].rearrange("p (h e) -> p h e", h=4), in0=kt[:].rearrange("p (h e) -> p h e", h=4), in1=sc[:, 20:24].unsqueeze(2).broadcast_to([128, 4, 64]), op=ALU.mult), r=[kt_b, sc_b], w=[kg_b])
                S.pe(lambda: [nc.tensor.matmul(psS2[:, c * 128:(c + 1) * 128], kg[:, c * 128:(c + 1) * 128], vn[:, c * 128:(c + 1) * 128], start=True, stop=True) for c in range(2)][-1], r=[kg_b, vn_b], w=[psS2_b])
                S.dve(lambda: nc.vector.tensor_tensor(out=tmpS[:], in0=psS2[:, 0:256].rearrange("p (c e) -> p c e", c=2), in1=blk3, op=ALU.mult), r=[psS2_b] + CB, w=[tmpS_b])
                for c in range(2):
                    S.dve(lambda c=c: nc.vector.scalar_tensor_tensor(out=Sst[:, c, :], in0=Sst[:, c, :], scalar=sc[:, 32 + c:33 + c], in1=tmpS[:, c, :], op0=ALU.mult, op1=ALU.add), r=[Sst_b, sc_b, tmpS_b], w=[Sst_b])
                S.act(lambda: nc.scalar.copy(out=Sbf[:], in_=Sst[:]), r=[Sst_b], w=[Sbf_b])
                if di == 0:
                    S.store(lambda tok=tok: nc.gpsimd.dma_start(out=OB1[tok:tok + 128, :], in_=o[:]), r=[o_b], w=G("OB1", tok, tok + 128))
                else:
                    ob1, ob1_b = PB["ob1"][n % 2]
                    sz, sz_b = PB["sz"][n % 2]
                    S.load(lambda ob1=ob1, tok=tok: nc.sync.dma_start(out=ob1[:], in_=OB1[tok:tok + 128, :]), r=G("OB1", tok, tok + 128), w=[ob1_b])
                    S.load(lambda sz=sz, tok=tok: nc.sync.dma_start(out=sz[:], in_=SZ[:, tok:tok + 128].rearrange("(c p) t -> p c t", c=2)), r=G("SZ", tok, tok + 128), w=[sz_b])
                    S.dve(lambda ob1=ob1: nc.vector.tensor_tensor(out=o[:], in0=o[:], in1=ob1[:], op=ALU.add), r=[o_b, ob1_b], w=[o_b])
                    on, on_b = PC["on"]
                    head_norm(o, o_b, on, on_b, PC["st"][0], PC["st"][1], PC["t2"][0], PC["t2"][1], False)
                    mx, mx_b = PC["mx"][n % 2]
                    pt_, pt_b_ = PSB[(n + 1) % 2]
                    S.pe(lambda pt_=pt_: [nc.tensor.transpose(pt_[:, c * 128:(c + 1) * 128], on[:, c * 128:(c + 1) * 128], ident_b) for c in range(2)][-1], r=[on_b] + CB, w=[pt_b_])
                    for c in range(2):
                        S.dve(lambda c=c, pt_=pt_, mx=mx, sz=sz: nc.vector.scalar_tensor_tensor(out=mx[:, c, :], in0=pt_[:, c * 128:(c + 1) * 128], scalar=col("gdn_norm"), in1=sz[:, c, :], op0=ALU.mult, op1=ALU.mult), r=[pt_b_, cols_b, sz_b], w=[mx_b])
                    S.store(lambda mx=mx, tok=tok: nc.gpsimd.dma_start(out=MIX[256:512, tok:tok + 128].rearrange("(c p) t -> p c t", c=2), in_=mx[:]), r=[mx_b], w=G("MIX2", tok, tok + 128) + G("MIX3", tok, tok + 128))
        phase_end()
        if stop == 'B':
            break
        if l == 0:
            H2T = dscr("H2T", [D, 2, SP], BF16)
            X2A = dscr("X2A", [T, D], F32)
            X2 = dscr("X2", [T, D], F32)
        if True:
            phase_begin()
            P3 = {}
            P3["mixt"] = [sbp("mixt%d" % i, [128, 8, 128], BF16) for i in range(2)]
            P3["x1"] = [sbp("x1_%d" % i, [128, D], F32) for i in range(2)]
            P3["h2T"] = [sbp("h2Tt%d" % i, [128, 8, 130], BF16) for i in range(2)]
            P3["h2blk"] = [sbp("h2blk%d" % i, [128, 8, 514], BF16) for i in range(2)]
            P3["gp"] = [sbp("gp%d" % i, [128, 514], F32) for i in range(2)]
            P3["cv"] = [sbp("cv%d" % i, [128, 512], F32) for i in range(2)]
            P3["actT"] = sbp("actT", [128, 11, 512], BF16)
            P3["pt"] = [sbp("p3pt%d" % i, [128, PLE], F32) for i in range(2)]
            P3["pb"] = sbp("p3pb", [128, PLE], BF16)
            P3["pT"] = sbp("p3pT", [128, 2, 128], BF16)
            P3["h3T"] = sbp("h3T", [128, 8, 128], BF16)
            P3["sg"] = [sbp("sg%d" % i, [128, 512], F32) for i in range(2)]
            phase_alloc_done()
        if l == 0:
            for c in range(8):
                S.store(lambda c=c: nc.gpsimd.dma_start(out=H2T[c * 128:(c + 1) * 128, :, 0:2], in_=zt16[:, 0:4].rearrange("p (a b) -> p a b", a=2)), r=[zt_b], w=GP("H2T", 0))
                S.store(lambda c=c: nc.gpsimd.dma_start(out=H2T[c * 128:(c + 1) * 128, :, SP - 2:SP], in_=zt16[:, 0:4].rearrange("p (a b) -> p a b", a=2)), r=[zt_b], w=GP("H2T", 1))
        wo = wbig[:, 0:8 * D].rearrange("p (k c) -> p k c", k=8)
        load_weight_bf16(lambda k, c0, cw: wo[:, k, c0:c0 + cw], lambda k, l=l: w_out_in[l, k * 128:(k + 1) * 128, :], D, None, wbig_b, 8)
        MIXN = ["MIX%d" % k for k in range(8)]
        for t in range(NT):
            tok = t * 128
            seg, off = divmod(tok, SEG)
            mixt, mixt_b = P3["mixt"][t % 2]
            xt, xt_b = xin_t[t % 2]
            x1, x1_b = P3["x1"][t % 2]
            S.load(lambda mixt=mixt, tok=tok: nc.sync.dma_start(out=mixt[:], in_=MIX[:, tok:tok + 128].rearrange("(k p) t -> p k t", k=8)), r=sum([G(nm, tok, tok + 128) for nm in MIXN], []), w=[mixt_b])
            S.load(lambda xt=xt, tok=tok: nc.sync.dma_start(out=xt[:], in_=xsrc[tok:tok + 128, :]), r=G(xsrc_name, tok, tok + 128), w=[xt_b])
            for hf in range(2):
                ps, ps_b = PSF[hf]
                S.pe(lambda ps=ps, mixt=mixt, hf=hf: [nc.tensor.matmul(ps[:], mixt[:, k, :], wo[:, k, hf * 512:(hf + 1) * 512], start=(k == 0), stop=(k == 7)) for k in range(8)][-1], r=[mixt_b, wbig_b], w=[ps_b])
                S.dve(lambda ps=ps, xt=xt, x1=x1, hf=hf: nc.vector.tensor_tensor(out=x1[:, hf * 512:(hf + 1) * 512], in0=ps[:], in1=xt[:, hf * 512:(hf + 1) * 512], op=ALU.add), r=[ps_b, xt_b], w=[x1_b])
            S.store(lambda x1=x1, tok=tok: nc.gpsimd.dma_start(out=X1[tok:tok + 128, :], in_=x1[:]), r=[x1_b], w=G("X1", tok, tok + 128))
            rmsnorm_tile(x1[:], [x1_b], hn_t[:], hn_b, 0)
            h2T, h2T_b = P3["h2T"][t % 2]
            transpose_to(lambda h2T=h2T: h2T[:, :, 0:128], h2T_b, hn_t, hn_b, 8, PSB[t % 2])
            S.store(lambda h2T=h2T, seg=seg, off=off: nc.gpsimd.dma_start(out=H2T[:, seg, 2 + off:2 + off + 128].rearrange("(k p) t -> p k t", k=8), in_=h2T[:, :, 0:128]), r=[h2T_b], w=G("H2T", tok, tok + 128))
            if seg == 1 and off == 0:
                S.act(lambda h2T=h2T: nc.scalar.activation(out=h2T[:, :, 128:129], in_=h2T[:, :, 0:1], func=AF.Identity, scale=flag[:, 0:1]), r=[h2T_b, flag_b], w=[h2T_b])
                S.store(lambda h2T=h2T: nc.gpsimd.dma_start(out=H2T[:, 0, SP - 2:SP - 1].rearrange("(k p) t -> p k t", k=8), in_=h2T[:, :, 128:129], allow_slow_non_contiguous=True), r=[h2T_b], w=GP("H2T", 2))
            if seg == 0 and off + 128 == SEG:
                S.act(lambda h2T=h2T: nc.scalar.activation(out=h2T[:, :, 128:129], in_=h2T[:, :, 127:128], func=AF.Identity, scale=flag[:, 0:1]), r=[h2T_b, flag_b], w=[h2T_b])
                S.store(lambda h2T=h2T: nc.gpsimd.dma_start(out=H2T[:, 1, 1:2].rearrange("(k p) t -> p k t", k=8), in_=h2T[:, :, 128:129], allow_slow_non_contiguous=True), r=[h2T_b], w=GP("H2T", 3))

        NJH = NJ // 2
        for half in range(2):
            wgv = wbig[:, 0:8 * NJH * 128].rearrange("p (k c) -> p k c", k=8)
            wuv = wbig[:, 8 * NJH * 128:16 * NJH * 128].rearrange("p (k c) -> p k c", k=8)
            wdv = wbig[:, 16 * NJH * 128:16 * NJH * 128 + NJH * D].rearrange("p (j c) -> p j c", j=NJH)
            j0 = half * NJH
            load_weight_bf16(lambda k, c0, cw: wgv[:, k, c0:c0 + cw], lambda k, l=l, j0=j0: wg_in[l, k * 128:(k + 1) * 128, j0 * 128:(j0 + NJH) * 128], NJH * 128, lambda k: col("norm2", k), wbig_b, 8)
            load_weight_bf16(lambda k, c0, cw: wuv[:, k, c0:c0 + cw], lambda k, l=l, j0=j0: wu_in[l, k * 128:(k + 1) * 128, j0 * 128:(j0 + NJH) * 128], NJH * 128, lambda k: col("norm2", k), wbig_b, 8)
            load_weight_bf16(lambda k, c0, cw: wdv[:, k, c0:c0 + cw], lambda k, l=l, j0=j0: wd_in[l, (j0 + k) * 128:(j0 + k + 1) * 128, :], D, None, wbig_b, NJH)
            xa_src, xa_name = (X1, "X1") if half == 0 else (X2A, "X2A")
            xa_dst, xa_dname = (X2A, "X2A") if half == 0 else (X2, "X2")
            actT, actT_b = P3["actT"]
            for g in range(NG):
                seg, off = divmod(g * 512, SEG)
                hb, hb_b = P3["h2blk"][g % 2]
                rb = G("H2T", max(g * 512 - 128, 0), min(g * 512 + 640, T)) + sum([GP("H2T", i) for i in range(4)], [])
                S.load(lambda hb=hb, seg=seg, off=off: nc.sync.dma_start(out=hb[:], in_=H2T[:, seg, 1 + off:1 + off + 514].rearrange("(k p) t -> p k t", k=8)), r=rb, w=[hb_b])
                for jj in range(NJH):
                    j = j0 + jj
                    psG, psG_b = PSF[0 + 3 * (jj % 2)]
                    psH, psH_b = PSF[1 + 3 * (jj % 2)]
                    psU, psU_b = PSF[2 + 3 * (jj % 2)]
                    gp, gp_b = P3["gp"][jj % 2]
                    cv, cv_b = P3["cv"][jj % 2]
                    S.pe(lambda psG=psG, hb=hb, jj=jj: [nc.tensor.matmul(psG[:], wgv[:, k, jj * 128:(jj + 1) * 128], hb[:, k, 1:513], start=(k == 0), stop=(k == 7)) for k in range(8)][-1], r=[wbig_b, hb_b], w=[psG_b])
                    S.pe(lambda psH=psH, hb=hb, jj=jj: [nc.tensor.matmul(psH[:, 0:2], wgv[:, k, jj * 128:(jj + 1) * 128], hb[:, k, 0:514:513], start=(k == 0), stop=(k == 7)) for k in range(8)][-1], r=[wbig_b, hb_b], w=[psH_b])
                    S.pe(lambda psU=psU, hb=hb, jj=jj: [nc.tensor.matmul(psU[:], wuv[:, k, jj * 128:(jj + 1) * 128], hb[:, k, 1:513], start=(k == 0), stop=(k == 7)) for k in range(8)][-1], r=[wbig_b, hb_b], w=[psU_b])
                    S.act(lambda gp=gp, psG=psG: nc.scalar.copy(out=gp[:, 1:513], in_=psG[:]), r=[psG_b], w=[gp_b])
                    S.act(lambda gp=gp, psH=psH: nc.scalar.copy(out=gp[:, 0:514:513], in_=psH[:, 0:2]), r=[psH_b], w=[gp_b])
                    S.dve(lambda gp=gp, cv=cv, j=j: nc.vector.tensor_scalar(out=cv[:], in0=gp[:, 0:512], scalar1=col("ffn_conv_w", 0 * NJ + j), scalar2=col("ffn_conv_b", j), op0=ALU.mult, op1=ALU.add), r=[gp_b, cols_b], w=[cv_b])
                    S.dve(lambda gp=gp, cv=cv, j=j: nc.vector.scalar_tensor_tensor(out=cv[:], in0=gp[:, 1:513], scalar=col("ffn_conv_w", 1 * NJ + j), in1=cv[:], op0=ALU.mult, op1=ALU.add), r=[gp_b, cols_b, cv_b], w=[cv_b])
                    S.dve(lambda gp=gp, cv=cv, j=j: nc.vector.scalar_tensor_tensor(out=cv[:], in0=gp[:, 2:514], scalar=col("ffn_conv_w", 2 * NJ + j), in1=cv[:], op0=ALU.mult, op1=ALU.add), r=[gp_b, cols_b, cv_b], w=[cv_b])
                    S.act(lambda cv=cv: nc.scalar.activation(out=cv[:], in_=cv[:], func=AF.Gelu_apprx_tanh), r=[cv_b], w=[cv_b])
                    S.dve(lambda cv=cv, psU=psU, jj=jj: nc.vector.tensor_tensor(out=actT[:, jj, :], in0=cv[:], in1=psU[:], op=ALU.mult), r=[cv_b, psU_b], w=[actT_b])
                for t in range(4):
                    tok = g * 512 + t * 128
                    xt, xt_b = xin_t[t % 2]
                    x1, x1_b = P3["x1"][t % 2]
                    S.load(lambda xt=xt, tok=tok: nc.sync.dma_start(out=xt[:], in_=xa_src[tok:tok + 128, :]), r=G(xa_name, tok, tok + 128), w=[xt_b])
                    for hf in range(2):
                        ps, ps_b = PSF[hf * 3 + 1]
                        S.pe(lambda ps=ps, t=t, hf=hf: [nc.tensor.matmul(ps[:], actT[:, jj, t * 128:(t + 1) * 128], wdv[:, jj, hf * 512:(hf + 1) * 512], start=(jj == 0), stop=(jj == NJH - 1)) for jj in range(NJH)][-1], r=[actT_b, wbig_b], w=[ps_b])
                        S.dve(lambda ps=ps, xt=xt, x1=x1, hf=hf: nc.vector.tensor_tensor(out=x1[:, hf * 512:(hf + 1) * 512], in0=ps[:], in1=xt[:, hf * 512:(hf + 1) * 512], op=ALU.add), r=[ps_b, xt_b], w=[x1_b])
                    S.store(lambda x1=x1, tok=tok: nc.gpsimd.dma_start(out=xa_dst[tok:tok + 128, :], in_=x1[:]), r=[x1_b], w=G(xa_dname, tok, tok + 128))

        pgv = wbig[:, 0:8 * D].rearrange("p (k c) -> p k c", k=8)
        ppv = wbig[:, 8 * D:10 * D].rearrange("p (k c) -> p k c", k=2)
        load_weight_bf16(lambda k, c0, cw: pgv[:, k, c0:c0 + cw], lambda k, l=l: pgate_in[l, k * 128:(k + 1) * 128, :], D, lambda k: col("norm3", k), wbig_b, 8)
        load_weight_bf16(lambda k, c0, cw: ppv[:, k, c0:c0 + cw], lambda k, l=l: pproj_in[l, k * 128:(k + 1) * 128, :], D, None, wbig_b, 2)
        for t in range(NT):
            tok = t * 128
            xt, xt_b = xin_t[t % 2]
            x1, x1_b = P3["x1"][t % 2]
            pt, pt_b = P3["pt"][t % 2]
            pb, pb_b = P3["pb"]
            pT, pT_b = P3["pT"]
            h3T, h3T_b = P3["h3T"]
            S.load(lambda xt=xt, tok=tok: nc.sync.dma_start(out=xt[:], in_=X2[tok:tok + 128, :]), r=G("X2", tok, tok + 128), w=[xt_b])
            S.load(lambda pt=pt, tok=tok, l=l: nc.sync.dma_start(out=pt[:], in_=p_in[l, tok:tok + 128, :]), w=[pt_b])
            rmsnorm_tile(xt[:], [xt_b], hn_t[:], hn_b, 0)
            transpose_to(lambda: h3T[:], h3T_b, hn_t, hn_b, 8, PSB[0])
            S.act(lambda pt=pt: nc.scalar.copy(out=pb[:], in_=pt[:]), r=[pt_b], w=[pb_b])
            transpose_to(lambda: pT[:], pT_b, pb, pb_b, 2, PSB[1])
            for hf in range(2):
                psA, psA_b = PSF[hf * 2]
                psB, psB_b = PSF[hf * 2 + 1]
                sg, sg_b = P3["sg"][hf]
                S.pe(lambda psA=psA, hf=hf: [nc.tensor.matmul(psA[:], h3T[:, k, :], pgv[:, k, hf * 512:(hf + 1) * 512], start=(k == 0), stop=(k == 7)) for k in range(8)][-1], r=[h3T_b, wbig_b], w=[psA_b])
                S.pe(lambda psB=psB, hf=hf: [nc.tensor.matmul(psB[:], pT[:, k, :], ppv[:, k, hf * 512:(hf + 1) * 512], start=(k == 0), stop=(k == 1)) for k in range(2)][-1], r=[pT_b, wbig_b], w=[psB_b])
                S.act(lambda sg=sg, psA=psA: nc.scalar.activation(out=sg[:], in_=psA[:], func=AF.Sigmoid), r=[psA_b], w=[sg_b])
                S.dve(lambda sg=sg, psB=psB: nc.vector.tensor_tensor(out=sg[:], in0=sg[:], in1=psB[:], op=ALU.mult), r=[sg_b, psB_b], w=[sg_b])
                S.dve(lambda sg=sg, xt=xt, x1=x1, hf=hf: nc.vector.tensor_tensor(out=x1[:, hf * 512:(hf + 1) * 512], in0=sg[:], in1=xt[:, hf * 512:(hf + 1) * 512], op=ALU.add), r=[sg_b, xt_b], w=[x1_b])
            S.store(lambda x1=x1, tok=tok: nc.gpsimd.dma_start(out=xdst[tok:tok + 128, :], in_=x1[:]), r=[x1_b], w=G(xdst_name, tok, tok + 128))
        phase_end()

    S.emit(stack)
    stack.close()
    return nc


def prep_inputs(inp, SEG, core_specs):
    perm = _win_cols()
    w_in_r = np.ascontiguousarray(inp["w_in"][:, :, perm])
    colsl = [_col_pack(inp, l)[0] for l in range(DEPTH)]
    cols = np.stack(colsl)
    small = np.stack([_small_pack(inp, l) for l in range(DEPTH)])
    lru_w = np.stack([np.stack([_lru_blockdiag(inp["lru_wr"][l]), _lru_blockdiag(inp["lru_wi"][l])]) for l in range(DEPTH)])
    shared = {
        "consts": CONST_ARR, "w_in": w_in_r, "w_out": np.ascontiguousarray(inp["w_out"]),
        "ffn_wg": np.ascontiguousarray(inp["ffn_wg"]), "ffn_wu": np.ascontiguousarray(inp["ffn_wu"]),
        "ffn_wd": np.ascontiguousarray(inp["ffn_wd"]), "ple_proj": np.ascontiguousarray(inp["ple_proj"]),
        "ple_gate": np.ascontiguousarray(inp["ple_gate"]), "lru_w": lru_w, "cols": cols, "small": small,
    }
    tabs = {}
    maps = []
    for x, p, linked in core_specs:
        if linked not in tabs:
            fm, tm = _rope_tables(SEG, linked)
            na = np.stack([_na_tables(np.asarray(inp["na_rpb"][l]), SEG, linked) for l in range(DEPTH)])
            tabs[linked] = (fm, tm, na.reshape(DEPTH, 9, 6, 128, NH * 128))
        fm, tm, na = tabs[linked]
        m = dict(shared)
        m.update({"x_in": np.ascontiguousarray(x, np.float32), "p_in": np.ascontiguousarray(p, np.float32),
                  "flag": np.full((128, 1), 1.0 if linked else 0.0, np.float32),
                  "rope_fm": fm, "rope_tm": tm, "natab": na})
        maps.append(m)
    return maps


def kernel(**inputs):
    inp = {k: np.asarray(v) for k, v in inputs.items()}
    SEG = 8192
    xp, xs = inp["x_prompt"], inp["x_sample"]
    pp, ps = inp["p_prompt"], inp["p_sample"]
    specs = [
        (np.concatenate([xp[0], xp[1]]), np.concatenate([pp[:, 0], pp[:, 1]], 1), False),
        (np.concatenate([xp[2], xp[3]]), np.concatenate([pp[:, 2], pp[:, 3]], 1), False),
        (xs[0], ps[:, 0], True),
    ]
    specs = specs + [specs[0]] * 5
    maps = prep_inputs(inp, SEG, specs)
    nc = build(SEG)
    res = run_bass_kernel_spmd(nc, maps, core_ids=list(range(8)))
    y = [np.asarray(r["y_out"]) for r in res.results]
    y_prompt = np.stack([y[0][:SEG], y[0][SEG:], y[1][:SEG], y[1][SEG:]]).astype(np.float32)
    y_sample = y[2][None].astype(np.float32)
    return (y_prompt, y_sample)
```
